# Optimizing a Trainium2 kernel written in Bass

```python
import jax, jax.numpy as jnp
from jax import lax
import numpy as np

D_MODEL = 1024
BATCH = 4
SEQ = 4096
DEPTH = 1

N_HEADS = 8
QK_NOPE_DIM = 128
QK_ROPE_DIM = 64
V_HEAD_DIM = 128
Q_LORA_RANK = 3 * D_MODEL // 8
KV_LORA_RANK = D_MODEL // 4
ROPE_THETA = 10000.0
Q_BLOCK = 128
CONV_CHANNELS = D_MODEL
CONV_WIDTH = 31
D_FF = 4 * D_MODEL
N_BRANCHES = 2
EPS = 1e-6
IN_SIZES = (Q_LORA_RANK, KV_LORA_RANK, QK_ROPE_DIM, 2 * CONV_CHANNELS, N_BRANCHES * D_MODEL)
IN_OFFSETS = tuple(int(o) for o in np.cumsum(IN_SIZES)[:-1])
D_IN = int(sum(IN_SIZES))

kernel_name = "hybrid_mla_conformer_conv_gated_block"


def rms_norm(x, g):
    x32 = x.astype(jnp.float32)
    y = x32 * lax.rsqrt(jnp.mean(x32 * x32, axis=-1, keepdims=True) + EPS)
    return y.astype(x.dtype) * g


def layer_norm(x, g, b):
    x32 = x.astype(jnp.float32)
    mu = jnp.mean(x32, axis=-1, keepdims=True)
    xc = x32 - mu
    var = jnp.mean(xc * xc, axis=-1, keepdims=True)
    y = xc * lax.rsqrt(var + EPS)
    return y.astype(x.dtype) * g + b


def rope_tables(positions):
    inv_freq = ROPE_THETA ** (-jnp.arange(0, QK_ROPE_DIM, 2, dtype=jnp.float32) / QK_ROPE_DIM)
    ang = positions.astype(jnp.float32)[..., None] * inv_freq
    return jnp.cos(ang), jnp.sin(ang)


def apply_rope(x, cos, sin):
    half = x.shape[-1] // 2
    x1, x2 = x[..., :half], x[..., half:]
    cos = cos.astype(x.dtype)
    sin = sin.astype(x.dtype)
    return jnp.concatenate([x1 * cos - x2 * sin, x2 * cos + x1 * sin], axis=-1)


def causal_block_attention(q_nope, q_rope, k_nope, k_rope, v):
    B, H, S, Dn = q_nope.shape
    Dr = q_rope.shape[-1]
    Dv = v.shape[-1]
    n_blk = S // Q_BLOCK
    qn = q_nope.reshape(B, H, n_blk, Q_BLOCK, Dn).transpose(2, 0, 1, 3, 4)
    qr = q_rope.reshape(B, H, n_blk, Q_BLOCK, Dr).transpose(2, 0, 1, 3, 4)
    scale = (Dn + Dr) ** -0.5
    key_idx = jnp.arange(S)

    def one_block(args):
        qn_b, qr_b, blk = args
        s = jnp.einsum('bhqd,bhkd->bhqk', qn_b, k_nope) + jnp.einsum('bhqd,bkd->bhqk', qr_b, k_rope)
        s = s.astype(jnp.float32) * scale
        q_idx = blk * Q_BLOCK + jnp.arange(Q_BLOCK)
        mask = key_idx[None, :] <= q_idx[:, None]
        s = jnp.where(mask, s, -jnp.inf)
        p = jax.nn.softmax(s, axis=-1).astype(v.dtype)
        return jnp.einsum('bhqk,bhkd->bhqd', p, v)

    out = lax.map(one_block, (qn, qr, jnp.arange(n_blk)))
    return out.transpose(1, 0, 3, 2, 4).reshape(B, S, H * Dv)


def hybrid_mixer(h, cos, sin, w_in, q_norm, w_uq, kv_norm, w_uk, w_uv, w_o_attn,
                 conv_w, conv_b, conv_ln_g, conv_ln_b, w_pw2, b_pw2, w_out):
    B, S, _ = h.shape
    z = h @ w_in
    c_q, c_kv, k_r, u_glu, gate_logits = jnp.split(z, IN_OFFSETS, axis=-1)

    q = (rms_norm(c_q, q_norm) @ w_uq).reshape(B, S, N_HEADS, QK_NOPE_DIM + QK_ROPE_DIM)
    q_nope = q[..., :QK_NOPE_DIM]
    q_rope = apply_rope(q[..., QK_NOPE_DIM:], cos[:, :, None, :], sin[:, :, None, :])
    c_kv_n = rms_norm(c_kv, kv_norm)
    k_nope = jnp.einsum('bsc,chd->bhsd', c_kv_n, w_uk)
    v = jnp.einsum('bsc,chd->bhsd', c_kv_n, w_uv)
    k_rope = apply_rope(k_r, cos, sin)
    attn = causal_block_attention(q_nope.transpose(0, 2, 1, 3), q_rope.transpose(0, 2, 1, 3),
                                  k_nope, k_rope, v)
    y_attn = attn @ w_o_attn

    a, b = jnp.split(u_glu, 2, axis=-1)
    u = a * jax.nn.sigmoid(b)
    u = lax.conv_general_dilated(u, conv_w, window_strides=(1,),
                                 padding=[(CONV_WIDTH - 1, 0)],
                                 dimension_numbers=('NWC', 'WIO', 'NWC'),
                                 feature_group_count=CONV_CHANNELS) + conv_b
    u = jax.nn.silu(layer_norm(u, conv_ln_g, conv_ln_b))
    y_conv = u @ w_pw2 + b_pw2

    g_attn, g_conv = jnp.split(gate_logits, N_BRANCHES, axis=-1)
    merged = jax.nn.sigmoid(g_attn) * y_attn + jax.nn.sigmoid(g_conv) * y_conv
    return merged @ w_out


def squared_relu_mlp(h, w_ff1, w_ff2):
    return jnp.square(jax.nn.relu(h @ w_ff1)) @ w_ff2


def setup_inputs(seed: int = 0) -> dict:
    key = jax.random.key(seed)
    ks = jax.random.split(key, 24)
    f32 = jnp.float32

    def nrm(k, shape, scale):
        return jax.random.normal(k, shape, f32) * scale

    def gain(k, shape):
        return 1.0 + 0.01 * jax.random.normal(k, shape, f32)

    L = DEPTH
    x = jax.random.normal(ks[0], (BATCH, SEQ, D_MODEL), f32)
    offset = jax.random.randint(ks[1], (BATCH, 1), 0, 1024, dtype=jnp.int32)
    positions = (offset + jnp.arange(SEQ, dtype=jnp.int32)[None, :]).astype(jnp.int32)
    return {
        "x": x,
        "positions": positions,
        "norm_mix_pre": gain(ks[2], (L, D_MODEL)),
        "w_in": nrm(ks[3], (L, D_MODEL, D_IN), D_MODEL ** -0.5),
        "q_norm": gain(ks[4], (L, Q_LORA_RANK)),
        "w_uq": nrm(ks[5], (L, Q_LORA_RANK, N_HEADS * (QK_NOPE_DIM + QK_ROPE_DIM)), Q_LORA_RANK ** -0.5),
        "kv_norm": gain(ks[6], (L, KV_LORA_RANK)),
        "w_uk": nrm(ks[7], (L, KV_LORA_RANK, N_HEADS, QK_NOPE_DIM), KV_LORA_RANK ** -0.5),
        "w_uv": nrm(ks[8], (L, KV_LORA_RANK, N_HEADS, V_HEAD_DIM), KV_LORA_RANK ** -0.5),
        "w_o_attn": nrm(ks[9], (L, N_HEADS * V_HEAD_DIM, D_MODEL), (N_HEADS * V_HEAD_DIM) ** -0.5),
        "conv_w": nrm(ks[10], (L, CONV_WIDTH, 1, CONV_CHANNELS), CONV_WIDTH ** -0.5),
        "conv_b": nrm(ks[11], (L, CONV_CHANNELS), 0.01),
        "conv_ln_g": gain(ks[12], (L, CONV_CHANNELS)),
        "conv_ln_b": nrm(ks[13], (L, CONV_CHANNELS), 0.01),
        "w_pw2": nrm(ks[14], (L, CONV_CHANNELS, D_MODEL), CONV_CHANNELS ** -0.5),
        "b_pw2": nrm(ks[15], (L, D_MODEL), 0.01),
        "w_out": nrm(ks[16], (L, D_MODEL, D_MODEL), D_MODEL ** -0.5),
        "norm_mix_post": gain(ks[17], (L, D_MODEL)),
        "norm_mlp_pre": gain(ks[18], (L, D_MODEL)),
        "w_ff1": nrm(ks[19], (L, D_MODEL, D_FF), D_MODEL ** -0.5),
        "w_ff2": nrm(ks[20], (L, D_FF, D_MODEL), D_FF ** -0.5),
        "norm_mlp_post": gain(ks[21], (L, D_MODEL)),
    }


def reference(x, positions, norm_mix_pre, w_in, q_norm, w_uq, kv_norm, w_uk, w_uv, w_o_attn,
              conv_w, conv_b, conv_ln_g, conv_ln_b, w_pw2, b_pw2, w_out, norm_mix_post,
              norm_mlp_pre, w_ff1, w_ff2, norm_mlp_post):
    cos, sin = rope_tables(positions)
    for l in range(DEPTH):
        h = rms_norm(x, norm_mix_pre[l])
        m = hybrid_mixer(h, cos, sin, w_in[l], q_norm[l], w_uq[l], kv_norm[l], w_uk[l], w_uv[l],
                         w_o_attn[l], conv_w[l], conv_b[l], conv_ln_g[l], conv_ln_b[l],
                         w_pw2[l], b_pw2[l], w_out[l])
        x = x + rms_norm(m, norm_mix_post[l])
        f = squared_relu_mlp(rms_norm(x, norm_mlp_pre[l]), w_ff1[l], w_ff2[l])
        x = x + rms_norm(f, norm_mlp_post[l])
    return x
```

```python
import os
import numpy as np
import concourse.bass as bass
import concourse.mybir as mybir
from concourse.bass_utils import run_bass_kernel_spmd

F32 = mybir.dt.float32
BF16 = mybir.dt.bfloat16
I32 = mybir.dt.int32
AF = mybir.ActivationFunctionType
ALU = mybir.AluOpType

D = 1024
S = 4096
B = 4
T = 2048
NBLK = 4
DQ, DKV, DR = 384, 256, 64
NH = 8
DFF = 4096
EPS = 1e-6
CW = 31
DIN = 4800
OFF_Q, OFF_KV, OFF_KR, OFF_GLU, OFF_GATE = 0, 384, 640, 704, 2752
SCALE = 192.0 ** -0.5
MAGIC = 12582912.0
TWO_PI = 6.283185307179586
CW1 = 6.28125
CW2 = TWO_PI - CW1
PI_SAFE = 3.1415925

V_QN, V_KVN, V_CB, V_LNG, V_LNB, V_BPW, V_CWT = 0, 3, 5, 13, 21, 29, 37
V_INVF = 37 + 8 * CW
V_PHC, V_PHS, V_EPS, V_HALO, V_KB, V_ZERO = V_INVF + 1, V_INVF + 2, V_INVF + 3, V_INVF + 4, V_INVF + 5, V_INVF + 21
NV = V_ZERO + 1

DEBUG = bool(int(os.environ.get("MK_DEBUG", "0")))


class Buf:
    __slots__ = ("name", "w", "r")

    def __init__(self, name, ghost=None):
        self.name = name
        self.w = None
        self.r = dict(ghost) if ghost else {}


def _merge(d, key, val):
    if d.get(key, -1) < val:
        d[key] = val


class Prog:
    ENG = ["pe", "act", "dve", "pool", "sp"]

    def __init__(self):
        self.ops = {e: [] for e in self.ENG}
        self.waited = {e: {} for e in self.ENG}
        self.dma_cnt = {}

    def _add(self, eng, fn, reads, writes, dma_sem=None):
        idx = len(self.ops[eng])
        deps = []
        for b in reads:
            if b.w is not None:
                deps.append((b.w, "raw"))
        for b in writes:
            if b.w is not None:
                deps.append((b.w, "waw"))
            for k, v in b.r.items():
                deps.append(((k, v), "war"))
        waits = []
        for (key, val), kind in deps:
            if key == eng:
                if eng == "pe":
                    continue
            if self.waited[eng].get(key, -1) >= val:
                continue
            self.waited[eng][key] = val
            waits.append((key, val))
        if dma_sem is None:
            ev = (eng, idx)
        else:
            cnt = self.dma_cnt.get(dma_sem, 0) + 16
            self.dma_cnt[dma_sem] = cnt
            ev = ("dma:" + dma_sem, cnt)
        self.ops[eng].append(dict(fn=fn, waits=waits, dma=dma_sem, inc=False))
        for b in reads:
            _merge(b.r, ev[0], ev[1])
        for b in writes:
            b.w = ev
            b.r = {}
        return ev

    def op(self, eng, fn, reads=(), writes=()):
        return self._add(eng, fn, list(reads), list(writes))

    def dma(self, queue, out, in_, reads=(), writes=(), sem=None):
        assert sem is not None
        return self._add(queue, lambda e: e.dma_start(out=out, in_=in_), list(reads), list(writes), dma_sem=sem)

    def mm(self, out, lhsT, rhs, start, stop, reads, writes):
        return self._add("pe", lambda e: e.matmul(out, lhsT, rhs, start=start, stop=stop), list(reads), list(writes))

    def tr(self, out, in_, ident, reads, writes):
        return self._add("pe", lambda e: e.transpose(out, in_, ident), list(reads), list(writes))

    def act(self, out, in_, func, reads, writes, **kw):
        return self._add("act", lambda e: e.activation(out, in_, func, **kw), list(reads), list(writes))

    def tt(self, eng, out, in0, in1, op, reads, writes):
        return self._add(eng, lambda e: e.tensor_tensor(out, in0, in1, op), list(reads), list(writes))

    def ts(self, eng, out, in0, s1, s2, op0, op1, reads, writes):
        if op1 is None:
            return self._add(eng, lambda e: e.tensor_scalar(out, in0, s1, None, op0), list(reads), list(writes))
        return self._add(eng, lambda e: e.tensor_scalar(out, in0, s1, s2, op0, op1), list(reads), list(writes))

    def stt(self, out, in0, scalar, in1, op0, op1, reads, writes):
        return self._add("dve", lambda e: e.scalar_tensor_tensor(out, in0, scalar, in1, op0, op1), list(reads), list(writes))

    def recip(self, out, in_, reads, writes):
        return self._add("dve", lambda e: e.reciprocal(out, in_), list(reads), list(writes))

    def copy(self, eng, out, in_, reads, writes):
        if eng == "act":
            return self._add("act", lambda e: e.activation(out, in_, AF.Copy), list(reads), list(writes))
        return self._add(eng, lambda e: e.tensor_copy(out, in_), list(reads), list(writes))

    def wait_all(self, eng, events):
        waits = []
        for key, val in events:
            if self.waited[eng].get(key, -1) >= val:
                continue
            self.waited[eng][key] = val
            waits.append((key, val))
        self.ops[eng].append(dict(fn=None, waits=waits, dma=None, inc=False))

    def emit(self, nc):
        for e in self.ENG:
            for o in self.ops[e]:
                for key, val in o["waits"]:
                    if not key.startswith("dma:"):
                        self.ops[key][val]["inc"] = True
        cum = {}
        for e in self.ENG:
            c = 0
            arr = []
            for o in self.ops[e]:
                if o["inc"]:
                    c += 1
                arr.append(c)
            cum[e] = arr
        sems = {}
        import contextlib
        with contextlib.ExitStack() as st:
            for e in ["pe", "act", "dve", "pool"]:
                sems[e] = st.enter_context(nc.semaphore("s_" + e))
            for k in self.dma_cnt:
                sems["dma:" + k] = st.enter_context(nc.semaphore("d_" + k))
            block = st.enter_context(nc.Block())

            def run(ename):
                def f(eng):
                    for o in self.ops[ename]:
                        for key, val in o["waits"]:
                            v = val if key.startswith("dma:") else cum[key][val]
                            eng.wait_ge(sems[key], v)
                        if o["fn"] is None:
                            continue
                        ins = o["fn"](eng)
                        if o["dma"] is not None:
                            ins.then_inc(sems["dma:" + o["dma"]], 16)
                        elif o["inc"]:
                            ins.then_inc(sems[ename], 1)
                return f

            block.tensor(run("pe"))
            block.scalar(run("act"))
            block.vector(run("dve"))
            block.gpsimd(run("pool"))
            block.sync(run("sp"))


class SBAlloc:
    def __init__(self, big, nbytes):
        self.big = big
        self.free = [(0, nbytes)]
        self.ghosts = []
        self.live = {}

    def alloc(self, name, shape, dtype, top=False):
        dsz = 2 if dtype == BF16 else 4
        n = 1
        for s in shape[1:]:
            n *= s
        nbytes = (n * dsz + 63) // 64 * 64
        idxs = range(len(self.free) - 1, -1, -1) if top else range(len(self.free))
        for i in idxs:
            o, sz = self.free[i]
            if sz >= nbytes:
                if sz == nbytes:
                    off = o
                    self.free.pop(i)
                elif top:
                    off = o + sz - nbytes
                    self.free[i] = (o, sz - nbytes)
                else:
                    off = o
                    self.free[i] = (o + nbytes, sz - nbytes)
                break
        else:
            raise RuntimeError(f"SBUF OOM allocating {name} {nbytes}; free={self.free}")
        ghost = {}
        keep = []
        for (go, gs, gev) in self.ghosts:
            if go < off + nbytes and off < go + gs:
                for k, v in gev.items():
                    _merge(ghost, k, v)
            keep.append((go, gs, gev))
        self.ghosts = keep
        ap = self.big[:, off // 2: off // 2 + (n * dsz) // 2]
        if dtype != BF16:
            ap = ap.bitcast(dtype)
        if len(shape) == 3:
            ap = ap.rearrange("p (a b) -> p a b", a=shape[1])
        elif len(shape) == 4:
            ap = ap.rearrange("p (a b c) -> p a b c", a=shape[1], b=shape[2])
        if shape[0] < 128:
            ap = ap[0:shape[0]]
        self.live[name] = (off, nbytes)
        return ap, ghost

    def release(self, name, bufs):
        off, nbytes = self.live.pop(name)
        ev = {}
        for b in bufs:
            if b.w is not None:
                _merge(ev, b.w[0], b.w[1])
            for k, v in b.r.items():
                _merge(ev, k, v)
        self.ghosts.append((off, nbytes, ev))
        self.free.append((off, nbytes))
        self.free.sort()
        merged = []
        for o, s in self.free:
            if merged and merged[-1][0] + merged[-1][1] == o:
                merged[-1] = (merged[-1][0], merged[-1][1] + s)
            else:
                merged.append((o, s))
        self.free = merged


class Tn:
    def __init__(self, sb, name, shape, dtype, nbuf=1, top=True):
        self.sb = sb
        self.name = name
        self.ap, ghost = sb.alloc(name, shape, dtype, top=top)
        self.b = [Buf(f"{name}.{i}", ghost) for i in range(nbuf)]

    def free(self):
        self.sb.release(self.name, self.b)


def build_program(dbg_list):
    nc = bass.Bass("TRN2", target_bir_lowering=False)
    P = Prog()

    def din(name, shape, dt=F32):
        return nc.dram_tensor(name, list(shape), dt, kind="ExternalInput").ap()

    x_own = din("x_own", [T, D])
    x_pre = din("x_pre", [T, D])
    pos = din("pos", [1, S], I32)
    vecs_d = din("vecs", [128, NV])
    cst_d = din("cst", [128, 256])
    g_pre_d = din("g_mix_pre", [1, D])
    g_post_d = din("g_mix_post", [1, D])
    g_pre2_d = din("g_mlp_pre", [1, D])
    g_post2_d = din("g_mlp_post", [1, D])
    w_in_d = din("w_in", [D, DIN]).rearrange("(c p) n -> p c n", p=128)
    wkrsw_d = din("wkr_sw", [D, DR]).rearrange("(c p) n -> p c n", p=128)
    wuq_d = din("w_uq", [DQ, NH * 192]).rearrange("(c p) n -> p c n", p=128)
    wuqsw_d = din("wuq_sw", [DQ, NH * 64]).rearrange("(c p) n -> p c n", p=128)
    wuk_d = din("w_uk", [DKV, NH * 128]).rearrange("(c p) n -> p c n", p=128)
    wuv_d = din("w_uv", [DKV, NH * 128]).rearrange("(c p) n -> p c n", p=128)
    wo_d = din("w_o", [D, D]).rearrange("(c p) n -> p c n", p=128)
    wpw2_d = din("w_pw2", [D, D]).rearrange("(c p) n -> p c n", p=128)
    wout_d = din("w_out", [D, D]).rearrange("(c p) n -> p c n", p=128)
    wff1_d = din("w_ff1", [D, DFF]).rearrange("(c p) n -> p c n", p=128)
    wff2_d = din("w_ff2", [DFF, D]).rearrange("(c p) n -> p c n", p=128)
    out_d = nc.dram_tensor("out", [T, D], F32, kind="ExternalOutput").ap()
    x1s_d = nc.dram_tensor("x1s", [T, D], F32, kind="Internal").ap()
    x1s_b = [Buf(f"x1s.{i}") for i in range(16)]

    SB_BYTES = 212000
    big = nc.alloc_sbuf_tensor("big", [128, SB_BYTES // 2], BF16)[:, :]
    sb = SBAlloc(big, SB_BYTES)
    ps_t = [nc.alloc_psum_tensor(f"ps{i}", [128, 512], F32) for i in range(8)]
    ps = [t[:, :] for t in ps_t]
    psb = [Buf(f"ps{i}") for i in range(8)]
    rr = {"i": 0}

    def nps(pool=range(8)):
        pool = list(pool)
        k = rr.get(tuple(pool), 0)
        rr[tuple(pool)] = k + 1
        i = pool[k % len(pool)]
        return ps[i], psb[i]

    evc = {"i": 0}

    def evac_eng():
        evc["i"] += 1
        return "act" if evc["i"] % 2 else "dve"

    def copy_op(eng, out, in_, reads, writes):
        if eng == "act":
            P.op("act", lambda e: e.activation(out, in_, AF.Copy), reads, writes)
        else:
            P.op(eng, lambda e: e.tensor_copy(out, in_), reads, writes)

    dbg_cnt = {"i": 0}

    def dbg(name, tn_ap, bufs, shape, dt=F32):
        if not DEBUG:
            return
        d = nc.dram_tensor("dbg_" + name, list(shape), dt, kind="ExternalOutput").ap()
        dbg_cnt["i"] += 1
        ev = P.dma("sp", d, tn_ap, reads=bufs, writes=[], sem=f"dbg{dbg_cnt['i']}")
        dbg_list.append(("dbg_" + name, ev))

    vecs = Tn(sb, "vecs", [128, NV], F32, top=False)
    P.dma("sp", vecs.ap, vecs_d, writes=vecs.b, sem="vecs")
    cst = Tn(sb, "cst", [128, 256], BF16, top=False)
    P.dma("pool", cst.ap, cst_d, writes=cst.b, sem="cst")
    ident = cst.ap[:, 0:128]
    cmask = cst.ap[:, 128:256]
    ones = Tn(sb, "ones", [128, 128], BF16, top=False)
    P._add("pool", lambda e: e.memset(ones.ap, 1.0), [], ones.b)
    V = vecs.ap
    VB = vecs.b

    def vcol(c, rows=128):
        return V[0:rows, c:c + 1]

    ssc = Tn(sb, "ssc", [128, 8], F32, nbuf=8, top=False)
    gA = Tn(sb, "gA", [128, D], F32, top=False)
    tA = Tn(sb, "tA", [128, 512], F32, top=False)
    tB = Tn(sb, "tB", [128, 512], F32, top=False)
    sqb = [Tn(sb, f"sqb{i}", [128, 512], BF16, top=False) for i in range(3)]
    rstd = [Tn(sb, f"rstd{i}", [128, 512], F32, top=False) for i in range(2)]
    hT_halo = Tn(sb, "hT_halo", [128, 8, 32], BF16, top=False)
    hT_own = Tn(sb, "hT_own", [128, 8, T], BF16, nbuf=4, top=False)
    ropeC = Tn(sb, "ropeC", [64, T], F32, nbuf=4, top=False)
    ropeS = Tn(sb, "ropeS", [64, T], F32, nbuf=4, top=False)
    ckvT = Tn(sb, "ckvT", [128, 2, S], BF16, nbuf=8, top=False)
    kropeT = Tn(sb, "kropeT", [128, S], BF16, nbuf=8, top=False)
    P._add("pool", lambda e: e.memset(kropeT.ap[64:128, :], 0.0), [], kropeT.b)
    cqT = Tn(sb, "cqT", [128, 3, T], BF16, nbuf=4, top=False)
    hT_pre = Tn(sb, "hT_pre", [128, 8, T], BF16, nbuf=4)
    ropeCp = Tn(sb, "ropeCp", [64, T], F32, nbuf=4)
    ropeSp = Tn(sb, "ropeSp", [64, T], F32, nbuf=4)

    wkv = Tn(sb, "wkv", [128, 8, 320], BF16)
    P.dma("pool", wkv.ap, w_in_d[:, :, OFF_KV:OFF_KV + 320], writes=wkv.b, sem="wkv")
    wkrsw = Tn(sb, "wkrsw", [128, 8, 64], BF16)
    P.dma("pool", wkrsw.ap, wkrsw_d, writes=wkrsw.b, sem="wkrsw")
    wq = Tn(sb, "wq", [128, 8, DQ], BF16)
    P.dma("pool", wq.ap, w_in_d[:, :, OFF_Q:OFF_Q + DQ], writes=wq.b, sem="wq")
    P.dma("sp", gA.ap, g_pre_d[0:1, :].partition_broadcast(128), writes=gA.b, sem="gA")

    RC = 1024
    posb = Tn(sb, "posb", [64, RC], I32)
    posf = Tn(sb, "posf", [64, RC], F32)
    rtmp = Tn(sb, "rtmp", [64, RC], F32)
    rang = Tn(sb, "rang", [64, RC], F32)
    posi = Tn(sb, "posi", [64, RC], F32)
    pending_sin = {}

    rope_q = []

    def queue_rope(ch):
        sl = slice(ch * RC, (ch + 1) * RC)
        lsl = slice((ch % 2) * RC, (ch % 2 + 1) * RC)
        rope_q.append((ch, lambda: P.dma("sp", posb.ap, pos[0:1, sl].partition_broadcast(64), writes=posb.b, sem="posb")))
        rope_q.append((ch, lambda: P.copy("dve", posf.ap, posb.ap, posb.b, posf.b)))
        tabs = (ropeCp, ropeSp) if ch < 2 else (ropeC, ropeS)
        sins = []
        for tab, phc in zip(tabs, (V_PHC, V_PHS)):
            tb = tab.b[(ch % 2) * 2:(ch % 2 + 1) * 2]
            dst = tab.ap[:, lsl]
            rope_q.append((ch, lambda phc=phc: P.ts("dve", rang.ap, posf.ap, vcol(V_INVF, 64), vcol(phc, 64), ALU.mult, ALU.add, posf.b + VB, rang.b)))
            rope_q.append((ch, lambda: P.ts("dve", rtmp.ap, rang.ap, 1.0 / TWO_PI, MAGIC, ALU.mult, ALU.add, rang.b, rtmp.b)))
            rope_q.append((ch, lambda: P.ts("dve", rtmp.ap, rtmp.ap, -MAGIC, None, ALU.add, None, rtmp.b, rtmp.b)))
            for cw_ in (CW1, CW2):
                rope_q.append((ch, lambda cw_=cw_: P.stt(rang.ap, rtmp.ap, -cw_, rang.ap, ALU.mult, ALU.add, rtmp.b + rang.b, rang.b)))
            rope_q.append((ch, lambda dst=dst, tb=tb: P.ts("dve", dst, rang.ap, -PI_SAFE, PI_SAFE, ALU.max, ALU.min, rang.b, tb)))
            sins.append((dst, tb))
        pending_sin[ch] = sins

    def drain_rope(nmax=None, upto=None):
        k = 0
        while rope_q and (nmax is None or k < nmax) and (upto is None or rope_q[0][0] <= upto):
            rope_q.pop(0)[1]()
            k += 1

    def flush_sin(ch):
        drain_rope(upto=ch)
        for dst, tb in pending_sin.pop(ch, []):
            P.act(dst, dst, AF.Sin, tb, tb)

    st = {}
    st["xt"] = [Tn(sb, f"xt{i}", [128, D], F32) for i in range(3)]
    st["xn"] = [Tn(sb, f"xn{i}", [128, D], BF16) for i in range(3)]
    st["junk"] = Tn(sb, "junk", [128, D], BF16)
    sc_i = {"i": 0}

    def scol():
        i = sc_i["i"] % 8
        sc_i["i"] += 1
        return ssc.ap[:, i:i + 1], ssc.b[i]

    def nt_front(src_ap, src_bufs, g_tn, k):
        junk = st["junk"]
        ss, ssb = scol()
        P.act(junk.ap, src_ap, AF.Square, src_bufs, junk.b + [ssb], accum_out=ss)
        P.act(ss, ss, AF.Ln, [ssb] + VB, [ssb], bias=vcol(V_EPS), scale=1.0 / D)
        P.act(ss, ss, AF.Exp, [ssb], [ssb], scale=-0.5)
        xnt = st["xn"][k % 3]
        P.stt(xnt.ap, src_ap, ss, g_tn.ap, ALU.mult, ALU.mult, list(src_bufs) + [ssb] + g_tn.b, xnt.b)

    def nt_back(dstT, dst_buf, tok0, k, pool=range(8)):
        xnt = st["xn"][k % 3]
        pa, pb = nps(pool)
        pav = pa.bitcast(BF16)
        for c in range(8):
            P.tr(pav[:, c * 128:(c + 1) * 128], xnt.ap[:, c * 128:(c + 1) * 128], ident, xnt.b + cst.b, [pb])
        P.copy(evac_eng(), dstT[:, :, tok0:tok0 + 128], pav.rearrange("p (c t) -> p c t", c=8), [pb], [dst_buf])

    def rstd_from_ss(bank, bankb, n, dst, dstb):
        P.act(dst, bank, AF.Ln, [bankb] + VB, [dstb], bias=vcol(V_EPS), scale=1.0 / n)
        P.act(dst, dst, AF.Exp, [dstb], [dstb], scale=-0.5)

    sq_i = {"i": 0}

    def next_sq():
        s_ = sqb[sq_i["i"] % 3]
        sq_i["i"] += 1
        return s_

    def lat_norm(banks, nchunk, gcol0, dstT, dst_buf, tsl, k):
        sqs = []
        for (ba, bb) in banks:
            s_ = next_sq()
            P.act(s_.ap, ba, AF.Square, [bb], s_.b)
            sqs.append(s_)
        sa, sbb = nps(MP)
        for j, s_ in enumerate(sqs):
            P.mm(sa, ones.ap, s_.ap, j == 0, j == nchunk - 1, s_.b + ones.b, [sbb])
        r = rstd[k % 2]
        rstd_from_ss(sa, sbb, nchunk * 128, r.ap, r.b[0])
        for j, (ba, bb) in enumerate(banks):
            P.stt(dstT[:, j, tsl], ba, vcol(gcol0 + j), r.ap, ALU.mult, ALU.mult, [bb] + r.b + VB, [dst_buf])

    def rope_apply(bank_r, bb_r, bank_s, bb_s, tC, tS, blk, dst, dst_buf):
        tsl_tab = slice(blk * 512, (blk + 1) * 512)
        P.tt("dve", tA.ap[0:64, :], bank_r[0:64, :], tC.ap[:, tsl_tab], ALU.mult, [bb_r, tC.b[blk]], tA.b)
        P.tt("dve", tB.ap[0:64, :], bank_s[0:64, :], tS.ap[:, tsl_tab], ALU.mult, [bb_s, tS.b[blk]], tB.b)
        P.tt("dve", dst, tA.ap[0:64, :], tB.ap[0:64, :], ALU.add, tA.b + tB.b, [dst_buf])

    tk = {"i": 0}
    MP = (0, 1, 2, 3, 4, 5)

    def tile_F(tk_):
        n, i = tk_ // 4, tk_ % 4
        own = n >= 4
        if tk_ == 0:
            queue_rope(0)
        if tk_ % 8 == 0 and tk_ // 8 + 1 < 4:
            queue_rope(tk_ // 8 + 1)
        src_d = x_own if own else x_pre
        t = (n % 4) * 4 + i
        xs = st["xt"][tk_ % 3]
        P.dma("sp", xs.ap, src_d[t * 128:(t + 1) * 128, :], writes=xs.b, sem=f"xt{tk_ % 3}")
        nt_front(xs.ap, xs.b, gA, tk_)
        drain_rope(nmax=3)

    def tile_B(tk_):
        n, i = tk_ // 4, tk_ % 4
        own = n >= 4
        hT = hT_own if own else hT_pre
        nb = n % 4
        t = nb * 4 + i
        nt_back(hT.ap, hT.b[nb], t * 128, tk_, pool=(6, 7))
        if tk_ == 15:
            P.copy("dve", hT_halo.ap, hT_pre.ap[:, :, T - 32:T], [hT_pre.b[3]], hT_halo.b)

    mstate = {}

    def M_kv_pe(n):
        own = n >= 4
        hT = hT_own if own else hT_pre
        nb = n % 4
        bsl = slice(nb * 512, (nb + 1) * 512)
        banks = []
        for j in range(2):
            pa, pb = nps(MP)
            for c in range(8):
                P.mm(pa, wkv.ap[:, c, j * 128:(j + 1) * 128], hT.ap[:, c, bsl], c == 0, c == 7, wkv.b + [hT.b[nb]], [pb])
            banks.append((pa, pb))
        pr, prb = nps(MP)
        for c in range(8):
            P.mm(pr[0:64, :], wkv.ap[:, c, 256:320], hT.ap[:, c, bsl], c == 0, c == 7, wkv.b + [hT.b[nb]], [prb])
        pq, pqb = nps(MP)
        for c in range(8):
            P.mm(pq[0:64, :], wkrsw.ap[:, c, :], hT.ap[:, c, bsl], c == 0, c == 7, wkrsw.b + [hT.b[nb]], [pqb])
        mstate[("kv", n)] = (banks, pr, prb, pq, pqb)

    def M_kv_post(n):
        own = n >= 4
        nb = n % 4
        asl = slice(n * 512, (n + 1) * 512)
        banks, pr, prb, pq, pqb = mstate.pop(("kv", n))
        flush_sin(n // 2)
        lat_norm(banks, 2, V_KVN, ckvT.ap, ckvT.b[n], asl, n)
        rope_apply(pr, prb, pq, pqb, ropeC if own else ropeCp, ropeS if own else ropeSp, nb, kropeT.ap[0:64, asl], kropeT.b[n])

    def M_q_pe(n):
        nb = n % 4
        bsl = slice(nb * 512, (nb + 1) * 512)
        banks = []
        for j in range(3):
            pa, pb = nps(MP)
            for c in range(8):
                P.mm(pa, wq.ap[:, c, j * 128:(j + 1) * 128], hT_own.ap[:, c, bsl], c == 0, c == 7, wq.b + [hT_own.b[nb]], [pb])
            banks.append((pa, pb))
        mstate[("q", n)] = banks

    def M_q_post(n):
        nb = n % 4
        bsl = slice(nb * 512, (nb + 1) * 512)
        lat_norm(mstate.pop(("q", n)), 3, V_QN, cqT.ap, cqT.b[nb], bsl, n + 1)

    sched = {}
    for n in range(8):
        base = 4 * n + 5
        sched.setdefault(base, []).append(lambda n=n: M_kv_pe(n))
        sched.setdefault(base + 1, []).append(lambda n=n: M_kv_post(n))
        if n >= 4:
            sched.setdefault(base + 1, []).append(lambda n=n: M_q_pe(n))
            sched.setdefault(base + 2, []).append(lambda n=n: M_q_post(n))
    tile_F(0)
    tile_F(1)
    for tk_ in range(32):
        if tk_ + 2 < 32:
            tile_F(tk_ + 2)
        tile_B(tk_)
        for f_ in sched.pop(tk_, []):
            f_()
    for k_ in sorted(sched):
        for f_ in sched[k_]:
            f_()
    dbg("hT_own", hT_own.ap, hT_own.b, [128, 8, T], BF16)
    dbg("ckvT", ckvT.ap, ckvT.b, [128, 2, S], BF16)
    dbg("kropeT", kropeT.ap[0:64, :], kropeT.b, [64, S], BF16)
    dbg("cqT", cqT.ap, cqT.b, [128, 3, T], BF16)
    hT_pre.free(); wkv.free(); wkrsw.free(); wq.free()
    for tn in st["xt"] + st["xn"] + [st["junk"], posb, posf, posi, rtmp, rang, ropeCp, ropeSp, gA] + sqb + rstd:
        tn.free()

    wuq = Tn(sb, "wuq", [128, 3, NH * 192], BF16)
    P.dma("pool", wuq.ap, wuq_d, writes=wuq.b, sem="wuq")
    wuqsw = Tn(sb, "wuqsw", [128, 3, NH * 64], BF16)
    P.dma("pool", wuqsw.ap, wuqsw_d, writes=wuqsw.b, sem="wuqsw")
    wuk = Tn(sb, "wuk", [128, 2, NH * 128], BF16)
    P.dma("pool", wuk.ap, wuk_d, writes=wuk.b, sem="wuk")
    wuv = Tn(sb, "wuv", [128, 2, NH * 128], BF16)
    P.dma("pool", wuv.ap, wuv_d, writes=wuv.b, sem="wuv")
    attnT = Tn(sb, "attnT", [128, NH, T], BF16, nbuf=NH * 4)
    KT2 = [Tn(sb, f"KT{i}", [128, S], BF16, nbuf=8) for i in range(2)]
    Vh2 = [Tn(sb, f"Vh{i}", [128, 32, 128], BF16, nbuf=8) for i in range(2)]
    QT2 = [Tn(sb, f"QT{i}", [128, T], BF16, nbuf=4) for i in range(2)]
    QrT2 = [Tn(sb, f"QrT{i}", [128, T], BF16, nbuf=4) for i in range(2)]
    for q_ in QrT2:
        P._add("pool", lambda e, q_=q_: e.memset(q_.ap[64:128, :], 0.0), [], q_.b)
    NPT = 6
    PT = [Tn(sb, f"PT{i}", [128, 512], BF16) for i in range(NPT)]
    rinv = Tn(sb, "rinv", [128, 512], F32)
    acc2 = [Tn(sb, f"acc{i}", [128, 512], F32) for i in range(2)]
    acch = Tn(sb, "acch", [128, 512], BF16)
    accl = Tn(sb, "accl", [128, 512], BF16)
    PREP = (6,)
    SBANK = (0, 1, 2)
    SUMB = (3, 7)

    def prep_groups(h, PREP=(6,)):
        KT, Vh, QT, QrT = KT2[h % 2], Vh2[h % 2], QT2[h % 2], QrT2[h % 2]
        gs = []

        def gK(n):
            pa, pb = nps(PREP)
            for j in range(2):
                P.mm(pa, wuk.ap[:, j, h * 128:(h + 1) * 128], ckvT.ap[:, j, n * 512:(n + 1) * 512], j == 0, j == 1, wuk.b + [ckvT.b[n]], [pb])
            P.copy("dve", KT.ap[:, n * 512:(n + 1) * 512], pa, [pb], [KT.b[n]])

        def gV(g):
            pa, pb = nps(PREP)
            for i in range(4):
                kt = g * 4 + i
                for j in range(2):
                    P.mm(pa[:, i * 128:(i + 1) * 128], ckvT.ap[:, j, kt * 128:(kt + 1) * 128], wuv.ap[:, j, h * 128:(h + 1) * 128],
                         j == 0, j == 1, wuv.b + [ckvT.b[g]], [pb])
            P.copy("dve", Vh.ap[:, g * 4:(g + 1) * 4, :], pa.rearrange("p (a b) -> p a b", a=4), [pb], [Vh.b[g]])

        def gQ(b_):
            bsl = slice(b_ * 512, (b_ + 1) * 512)
            pa, pb = nps(PREP)
            for j in range(3):
                P.mm(pa, wuq.ap[:, j, h * 192:h * 192 + 128], cqT.ap[:, j, bsl], j == 0, j == 2, wuq.b + [cqT.b[b_]], [pb])
            P.copy("dve", QT.ap[:, bsl], pa, [pb], [QT.b[b_]])

        def gQr(b_, hb):
            hsl = slice(b_ * 512 + hb * 256, b_ * 512 + (hb + 1) * 256)
            pa, pb = nps(PREP)
            for j in range(3):
                P.mm(pa[0:64, 0:256], wuq.ap[:, j, h * 192 + 128:h * 192 + 192], cqT.ap[:, j, hsl], j == 0, j == 2, wuq.b + [cqT.b[b_]], [pb])
            for j in range(3):
                P.mm(pa[0:64, 256:512], wuqsw.ap[:, j, h * 64:(h + 1) * 64], cqT.ap[:, j, hsl], j == 0, j == 2, wuqsw.b + [cqT.b[b_]], [pb])
            P.tt("dve", tA.ap[0:64, 0:256], pa[0:64, 0:256], ropeC.ap[:, hsl], ALU.mult, [pb, ropeC.b[b_]], tA.b)
            P.tt("dve", tB.ap[0:64, 0:256], pa[0:64, 256:512], ropeS.ap[:, hsl], ALU.mult, [pb, ropeS.b[b_]], tB.b)
            P.tt("dve", QrT.ap[0:64, hsl], tA.ap[0:64, 0:256], tB.ap[0:64, 0:256], ALU.add, tA.b + tB.b, [QrT.b[b_]])

        for n in range(8):
            gs.append(lambda n=n: gK(n))
            gs.append(lambda n=n: gV(n))
        for b_ in range(4):
            gs.append(lambda b_=b_: gQ(b_))
            gs.append(lambda b_=b_: gQr(b_, 0))
            gs.append(lambda b_=b_: gQr(b_, 1))
        return gs

    its = []
    for h in range(NH):
        for qb in range(4):
            nfull = 16 + 4 * qb
            order = [0] + [nfull + j for j in range(4)] + list(range(1, nfull))
            for ii, kt in enumerate(order):
                its.append(dict(h=h, qb=qb, kt=kt, j=kt - nfull, ii=ii, last=(ii == len(order) - 1)))
    for k_, it in enumerate(its):
        it["pt"] = PT[k_ % NPT]
        it["s"] = SBANK[k_ % 3]

    def emit_S(it):
        h, qb, kt, j = it["h"], it["qb"], it["kt"], it["j"]
        KT, QT, QrT = KT2[h % 2], QT2[h % 2], QrT2[h % 2]
        q0 = j * 128 if j > 0 else 0
        nq = 512 - q0
        qsl = slice(qb * 512 + q0, qb * 512 + 512)
        sa, sbf = ps[it["s"]], psb[it["s"]]
        P.mm(sa[:, 0:nq], KT.ap[:, kt * 128:(kt + 1) * 128], QT.ap[:, qsl], True, False, [KT.b[kt // 4], QT.b[qb]], [sbf])
        P.mm(sa[:, 0:nq], kropeT.ap[:, kt * 128:(kt + 1) * 128], QrT.ap[:, qsl], False, True, [kropeT.b[kt // 4], QrT.b[qb]], [sbf])
        pt = it["pt"]
        bcol = vcol(V_KB + kt) if kt < 16 else vcol(V_ZERO)
        P.act(pt.ap[:, 0:nq], sa[:, 0:nq], AF.Exp, [sbf] + VB, pt.b, bias=bcol, scale=SCALE)
        if j >= 0:
            P.tt("dve", pt.ap[:, 0:128], pt.ap[:, 0:128], cmask, ALU.mult, pt.b + cst.b, pt.b)

    deferred = []

    def emit_SP(it):
        h, qb, kt, j, ii = it["h"], it["qb"], it["kt"], it["j"], it["ii"]
        Vh = Vh2[h % 2]
        q0 = j * 128 if j > 0 else 0
        nq = 512 - q0
        pt = it["pt"]
        psum_a, psum_b = ps[SUMB[qb % 2]], psb[SUMB[qb % 2]]
        po_a, po_b = ps[4 + qb % 2], psb[4 + qb % 2]
        acc = acc2[qb % 2]
        if ii % 4 != 1 or ii < 5:
            if ii == 0:
                P.copy("dve", acc.ap, pt.ap, pt.b, acc.b)
            else:
                P.tt("dve", acc.ap[:, q0:512], acc.ap[:, q0:512], pt.ap[:, 0:nq], ALU.add, acc.b + pt.b, acc.b)
        else:
            P.mm(psum_a[:, q0:512], ones.ap, pt.ap[:, 0:nq], ii == 5, False, pt.b + ones.b, [psum_b])
        P.mm(po_a[:, q0:512], Vh.ap[:, kt, :], pt.ap[:, 0:nq], ii == 0, it["last"], pt.b + [Vh.b[kt // 4]], [po_b])
        if it["last"]:
            def fin_dve(acc=acc):
                P.copy("dve", acch.ap, acc.ap, acc.b, acch.b)
                P.stt(accl.ap, acch.ap, -1.0, acc.ap, ALU.mult, ALU.add, acch.b + acc.b, accl.b)

            def fin(h=h, qb=qb, acc=acc, psum_a=psum_a, psum_b=psum_b, po_a=po_a, po_b=po_b):
                P.mm(psum_a, ones.ap, acch.ap, False, False, acch.b + ones.b, [psum_b])
                P.mm(psum_a, ones.ap, accl.ap, False, True, accl.b + ones.b, [psum_b])
                P.act(rinv.ap, psum_a, AF.Ln, [psum_b], rinv.b)
                P.act(rinv.ap, rinv.ap, AF.Exp, rinv.b, rinv.b, scale=-1.0)
                P.tt("dve", attnT.ap[:, h, qb * 512:(qb + 1) * 512], po_a, rinv.ap, ALU.mult, [po_b] + rinv.b, [attnT.b[h * 4 + qb]])
            deferred.append([3, fin_dve])
            deferred.append([9, fin])

    for g_ in prep_groups(0, PREP=(0, 1, 2, 3, 4, 5, 6, 7)):
        g_()
    if DEBUG:
        dbg("KT0", KT2[0].ap, KT2[0].b, [128, S], BF16)
        dbg("Vh0", Vh2[0].ap, Vh2[0].b, [128, 32, 128], BF16)
        dbg("QT0", QT2[0].ap, QT2[0].b, [128, T], BF16)
        dbg("QrT0", QrT2[0].ap[0:64, :], QrT2[0].b, [64, T], BF16)
    LOOK = 3
    pend = []
    cur_h = -1
    hcount = 0
    for k_ in range(min(LOOK, len(its))):
        emit_S(its[k_])
    for k_, it in enumerate(its):
        if it["h"] != cur_h:
            cur_h = it["h"]
            hcount = 0
            pend = prep_groups(cur_h + 1) if cur_h + 1 < NH else []
        emit_SP(it)
        if k_ + LOOK < len(its):
            emit_S(its[k_ + LOOK])
        for d_ in list(deferred):
            d_[0] -= 1
            if d_[0] <= 0:
                deferred.remove(d_)
                d_[1]()
        hcount += 1
        if pend and hcount >= 4 and hcount % 3 == 1:
            pend.pop(0)()
        if hcount == 100:
            while pend:
                pend.pop(0)()
    for d_ in deferred:
        d_[1]()
    dbg("attnT", attnT.ap, attnT.b, [128, NH, T], BF16)
    for tn in [wuq, wuqsw, wuk, wuv, rinv, ckvT, kropeT, cqT, ropeC, ropeS, acch, accl] + acc2 + KT2 + Vh2 + QT2 + QrT2 + PT:
        tn.free()

    wo = Tn(sb, "wo", [128, 8, D], BF16)
    P.dma("pool", wo.ap, wo_d, writes=wo.b, sem="wo")
    yaT = Tn(sb, "yaT", [128, 8, T], BF16, nbuf=32)
    for b_ in range(4):
        bsl = slice(b_ * 512, (b_ + 1) * 512)
        for dc in range(8):
            pa, pb = nps()
            for h in range(NH):
                P.mm(pa, wo.ap[:, h, dc * 128:(dc + 1) * 128], attnT.ap[:, h, bsl], h == 0, h == NH - 1, wo.b + [attnT.b[h * 4 + b_]], [pb])
            P.copy(evac_eng(), yaT.ap[:, dc, bsl], pa, [pb], [yaT.b[dc * 4 + b_]])
    attnT.free(); wo.free()
    dbg("yaT", yaT.ap, yaT.b, [128, 8, T], BF16)

    vT = Tn(sb, "vT", [128, 8, T], BF16, nbuf=32)
    uT = [Tn(sb, f"uT{i}", [128, 32 + T], BF16, nbuf=5) for i in range(2)]
    wglu = [Tn(sb, f"wglu{i}", [128, 8, 256], BF16) for i in range(2)]
    diag = [Tn(sb, f"diag{i}", [128, CW, 128], BF16) for i in range(2)]
    sg = [Tn(sb, f"sg{i}", [128, 512], F32) for i in range(2)]
    cacc = [Tn(sb, f"cacc{i}", [128, 512], F32) for i in range(2)]
    NDVE = 6
    for cc in range(8):
        k = cc % 2
        wg, ut, dg = wglu[k], uT[k], diag[k]
        P.dma("pool", wg.ap[:, :, 0:128], w_in_d[:, :, OFF_GLU + cc * 128:OFF_GLU + (cc + 1) * 128], writes=wg.b, sem=f"wglu{k}")
        P.dma("pool", wg.ap[:, :, 128:256], w_in_d[:, :, OFF_GLU + D + cc * 128:OFF_GLU + D + (cc + 1) * 128], writes=wg.b, sem=f"wglu{k}")
        c0 = V_CWT + cc * CW
        P.tt("dve", dg.ap, ident.unsqueeze(1).to_broadcast([128, CW, 128]), V[:, c0:c0 + CW].unsqueeze(2).to_broadcast([128, CW, 128]),
             ALU.mult, cst.b + VB, dg.b)
        pa, pb = nps()
        for half in range(2):
            for c in range(8):
                P.mm(pa[:, half * 32:(half + 1) * 32], wg.ap[:, c, half * 128:(half + 1) * 128], hT_halo.ap[:, c, :], c == 0, c == 7,
                     wg.b + hT_halo.b, [pb])
        s0 = sg[0]
        P.act(s0.ap[:, 0:32], pa[:, 32:64], AF.Sigmoid, [pb], s0.b)
        P.stt(ut.ap[:, 0:32], pa[:, 0:32], vcol(V_HALO), s0.ap[:, 0:32], ALU.mult, ALU.mult, [pb] + s0.b + VB, [ut.b[4]])
        for b_ in range(4):
            bsl = slice(b_ * 512, (b_ + 1) * 512)
            pa, pb = nps()
            pg, pgb = nps()
            for c in range(8):
                P.mm(pa, wg.ap[:, c, 0:128], hT_own.ap[:, c, bsl], c == 0, c == 7, wg.b + [hT_own.b[b_]], [pb])
            for c in range(8):
                P.mm(pg, wg.ap[:, c, 128:256], hT_own.ap[:, c, bsl], c == 0, c == 7, wg.b + [hT_own.b[b_]], [pgb])
            s_ = sg[b_ % 2]
            P.act(s_.ap, pg, AF.Sigmoid, [pgb], s_.b)
            P.tt("dve", ut.ap[:, 32 + b_ * 512:32 + (b_ + 1) * 512], pa, s_.ap, ALU.mult, [pb] + s_.b, [ut.b[b_]])
        for b_ in range(4):
            pa, pb = nps()
            rd = [ut.b[b_], ut.b[b_ - 1] if b_ > 0 else ut.b[4]]
            ca = cacc[b_ % 2]
            for j in range(NDVE):
                o = 32 + b_ * 512 - (CW - 1) + j
                wcol = vcol(V_CWT + cc * CW + j)
                if j == 0:
                    P.ts("dve", ca.ap, ut.ap[:, o:o + 512], wcol, None, ALU.mult, None, rd + VB, ca.b)
                else:
                    P.stt(ca.ap, ut.ap[:, o:o + 512], wcol, ca.ap, ALU.mult, ALU.add, rd + VB + ca.b, ca.b)
            for j in range(NDVE, CW):
                o = 32 + b_ * 512 - (CW - 1) + j
                P.mm(pa, dg.ap[:, j, :], ut.ap[:, o:o + 512], j == NDVE, j == CW - 1, dg.b + rd, [pb])
            P.stt(vT.ap[:, cc, b_ * 512:(b_ + 1) * 512], pa, vcol(V_CB + cc), ca.ap, ALU.add, ALU.add, [pb] + VB + ca.b, [vT.b[cc * 4 + b_]])
    dbg("yconv", vT.ap, vT.b, [128, 8, T], BF16)
    for tn in uT + wglu + diag + cacc:
        tn.free()
    wpw2 = Tn(sb, "wpw2", [128, 8, D], BF16)
    P.dma("pool", wpw2.ap, wpw2_d, writes=wpw2.b, sem="wpw2")
    wout = Tn(sb, "wout", [128, 8, D], BF16)
    meanf = [Tn(sb, f"meanf{i}", [128, 512], F32) for i in range(2)]
    sqb = [Tn(sb, f"sqb{i}", [128, 512], BF16) for i in range(3)]
    rstd = [Tn(sb, f"rstd{i}", [128, 512], F32) for i in range(2)]
    ln_r = {}

    def ln_stats(b_):
        bsl = slice(b_ * 512, (b_ + 1) * 512)
        p1, p1b = nps()
        p2, p2b = nps()
        for cc in range(8):
            s_ = next_sq()
            P.act(s_.ap, vT.ap[:, cc, bsl], AF.Square, [vT.b[cc * 4 + b_]], s_.b)
            P.mm(p1, ones.ap, vT.ap[:, cc, bsl], cc == 0, cc == 7, ones.b + [vT.b[cc * 4 + b_]], [p1b])
            P.mm(p2, ones.ap, s_.ap, cc == 0, cc == 7, ones.b + s_.b, [p2b])
        mf = meanf[b_ % 2]
        P.act(mf.ap, p1, AF.Identity, [p1b], mf.b, scale=1.0 / D)
        P.tt("dve", tA.ap, mf.ap, mf.ap, ALU.mult, mf.b, tA.b)
        r = rstd[b_ % 2]
        P.stt(r.ap, p2, 1.0 / D, tA.ap, ALU.mult, ALU.subtract, [p2b] + tA.b, r.b)
        P.act(r.ap, r.ap, AF.Ln, r.b + VB, r.b, bias=vcol(V_EPS))
        P.act(r.ap, r.ap, AF.Exp, r.b, r.b, scale=-0.5)

    def ln_apply(b_):
        bsl = slice(b_ * 512, (b_ + 1) * 512)
        mf, r = meanf[b_ % 2], rstd[b_ % 2]
        for cc in range(8):
            tt_ = sg[cc % 2]
            P.tt("dve", tt_.ap, vT.ap[:, cc, bsl], mf.ap, ALU.subtract, [vT.b[cc * 4 + b_]] + mf.b, tt_.b)
            P.tt("dve", tt_.ap, tt_.ap, r.ap, ALU.mult, tt_.b + r.b, tt_.b)
            P.act(vT.ap[:, cc, bsl], tt_.ap, AF.Silu, tt_.b + VB, [vT.b[cc * 4 + b_]], bias=vcol(V_LNB + cc), scale=vcol(V_LNG + cc))

    pf_state = {}
    def prefetch_wout():
        if pf_state:
            return
        pf_state["done"] = True
        P.dma("pool", wout.ap, wout_d, writes=wout.b, sem="wout")
        P.dma("sp", gB.ap, g_post_d[0:1, :].partition_broadcast(128), writes=gB.b, sem="gB")
        P.dma("sp", gA.ap, g_pre2_d[0:1, :].partition_broadcast(128), writes=gA.b, sem="gA")
    wga = Tn(sb, "wga", [128, 8, D], BF16)
    P.dma("pool", wga.ap, w_in_d[:, :, OFF_GATE:OFF_GATE + D], writes=wga.b, sem="wga")
    wgc = Tn(sb, "wgc", [128, 8, D], BF16)
    P.dma("pool", wgc.ap, w_in_d[:, :, OFF_GATE + D:OFF_GATE + 2 * D], writes=wgc.b, sem="wgc")
    sA = [Tn(sb, f"sA{i}", [128, 512], F32) for i in range(2)]
    sC = [Tn(sb, f"sC{i}", [128, 512], F32) for i in range(2)]
    gB = Tn(sb, "gB", [128, D], F32)
    gA = Tn(sb, "gA", [128, D], F32)
    prefetch_wout()

    def merge_block(b_):
        bsl = slice(b_ * 512, (b_ + 1) * 512)
        for dc in range(8):
            pga, pgab = nps()
            pgc, pgcb = nps()
            ppw, ppwb = nps()
            for c in range(8):
                P.mm(pga, wga.ap[:, c, dc * 128:(dc + 1) * 128], hT_own.ap[:, c, bsl], c == 0, c == 7, wga.b + [hT_own.b[b_]], [pgab])
            for c in range(8):
                P.mm(pgc, wgc.ap[:, c, dc * 128:(dc + 1) * 128], hT_own.ap[:, c, bsl], c == 0, c == 7, wgc.b + [hT_own.b[b_]], [pgcb])
            for c in range(8):
                P.mm(ppw, wpw2.ap[:, c, dc * 128:(dc + 1) * 128], vT.ap[:, c, bsl], c == 0, c == 7, wpw2.b + [vT.b[c * 4 + b_]], [ppwb])
            sa_, sc_ = sA[dc % 2], sC[dc % 2]
            P.act(sa_.ap, pga, AF.Sigmoid, [pgab], sa_.b)
            P.act(sc_.ap, pgc, AF.Sigmoid, [pgcb], sc_.b)
            P.stt(sc_.ap, ppw, vcol(V_BPW + dc), sc_.ap, ALU.add, ALU.mult, [ppwb] + sc_.b + VB, sc_.b)
            yb = yaT.b[dc * 4 + b_]
            P.tt("dve", sa_.ap, sa_.ap, yaT.ap[:, dc, bsl], ALU.mult, sa_.b + [yb], sa_.b)
            P.tt("dve", yaT.ap[:, dc, bsl], sa_.ap, sc_.ap, ALU.add, sa_.b + sc_.b, [yb])

    ln_stats(0)
    ln_stats(1)
    ln_apply(0)
    for b_ in range(4):
        if b_ + 2 < 4:
            ln_stats(b_ + 2)
        if b_ + 1 < 4:
            ln_apply(b_ + 1)
        merge_block(b_)
    dbg("vT", vT.ap, vT.b, [128, 8, T], BF16)
    dbg("mergedT", yaT.ap, yaT.b, [128, 8, T], BF16)
    for tn in [vT, wpw2, wga, wgc, hT_own, hT_halo] + meanf + sA + sC + sg + sqb + rstd:
        tn.free()

    h2T = Tn(sb, "h2T", [128, 8, T], BF16, nbuf=16)
    st["xt"] = [Tn(sb, f"xt{i}", [128, D], F32) for i in range(3)]
    st["xn"] = [Tn(sb, f"xn{i}", [128, D], BF16) for i in range(3)]
    st["junk"] = Tn(sb, "junk", [128, D], BF16)
    w1 = [Tn(sb, f"w1_{q}", [128, 8, 1024], BF16) for q in range(4)]
    for q in range(4):
        P.dma("pool", w1[q].ap, wff1_d[:, :, q * 1024:(q + 1) * 1024], writes=w1[q].b, sem=f"w1_{q}")
    x1t = [Tn(sb, f"x1t{i}", [128, D], F32) for i in range(4)]

    def post_norm_residual(pA, pAb, pB, pBb, g_tn, res_ap, res_bufs, dst):
        junk = st["junk"]
        s1, s1b = scol()
        s2, s2b = scol()
        P.act(junk.ap[:, 0:512], pA, AF.Square, [pAb], junk.b + [s1b], accum_out=s1)
        P.act(junk.ap[:, 512:1024], pB, AF.Square, [pBb], junk.b + [s2b], accum_out=s2)
        P.tt("dve", s1, s1, s2, ALU.add, [s1b, s2b], [s1b])
        P.act(s1, s1, AF.Ln, [s1b] + VB, [s1b], bias=vcol(V_EPS), scale=1.0 / D)
        P.act(s1, s1, AF.Exp, [s1b], [s1b], scale=-0.5)
        P.stt(dst.ap[:, 0:512], pA, s1, g_tn.ap[:, 0:512], ALU.mult, ALU.mult, [pAb, s1b] + g_tn.b, dst.b)
        P.stt(dst.ap[:, 512:1024], pB, s1, g_tn.ap[:, 512:1024], ALU.mult, ALU.mult, [pBb, s1b] + g_tn.b, dst.b)
        P.tt("pool", dst.ap, dst.ap, res_ap, ALU.add, dst.b + list(res_bufs), dst.b)

    def wo_A(tt):
        tsl = slice(tt * 128, (tt + 1) * 128)
        pA, pAb = nps((0, 1, 2, 3, 4, 5))
        pB, pBb = nps((0, 1, 2, 3, 4, 5))
        for half, (pp, ppb) in enumerate(((pA, pAb), (pB, pBb))):
            for c in range(8):
                P.mm(pp, yaT.ap[:, c, tsl], wout.ap[:, c, half * 512:(half + 1) * 512], c == 0, c == 7, wout.b + [yaT.b[c * 4 + tt // 4]], [ppb])
        xs = st["xt"][tt % 3]
        x1 = x1t[tt % 4]
        post_norm_residual(pA, pAb, pB, pBb, gB, xs.ap, xs.b, x1)
        P.dma("sp", x1s_d[tsl, :], x1.ap, reads=x1.b, writes=[x1s_b[tt]], sem=f"x1s{tt % 4}")

    def wo_F(tt):
        x1 = x1t[tt % 4]
        nt_front(x1.ap, x1.b, gA, tt)

    def wo_B(tt):
        nt_back(h2T.ap, h2T.b[tt], tt * 128, tt, pool=(6, 7))

    def wo_load(tt):
        xs = st["xt"][tt % 3]
        P.dma("sp", xs.ap, x_own[tt * 128:(tt + 1) * 128, :], writes=xs.b, sem=f"xt{tt % 3}")

    wo_load(0)
    wo_load(1)
    for s_ in range(16 + 3):
        if s_ + 2 < 16:
            wo_load(s_ + 2)
        if s_ < 16:
            wo_A(s_)
        if 0 <= s_ - 2 < 16:
            wo_F(s_ - 2)
        if 0 <= s_ - 3 < 16:
            wo_B(s_ - 3)
    dbg("h2T", h2T.ap, h2T.b, [128, 8, T], BF16)
    yaT.free(); wout.free()
    for tn in x1t + st["xn"] + [tA, tB, gA]:
        tn.free()

    w2 = [Tn(sb, f"w2_{q}", [128, 8, D], BF16) for q in range(4)]
    for q in range(4):
        P.dma("pool", w2[q].ap, wff2_d[:, q * 8:(q + 1) * 8, :], writes=w2[q].b, sem=f"w2_{q}")
    P.dma("sp", gB.ap, g_post2_d[0:1, :].partition_broadcast(128), writes=gB.b, sem="gB")
    f1T = [Tn(sb, f"f1T{i}", [128, 8, 256], BF16, nbuf=4) for i in range(2)]
    rl = [Tn(sb, f"rl{i}", [128, 512], F32) for i in range(2)]
    ot = [Tn(sb, f"ot{i}", [128, D], F32) for i in range(2)]
    ACC = (0, 1, 2, 3)
    FF1 = (4, 5, 6, 7)
    out_events = []
    rl_i = 0
    def ff1(fb, q):
        fsl = slice(fb * 256, (fb + 1) * 256)
        f1 = f1T[q % 2]
        for hp in range(4):
            pa, pb = nps(FF1)
            for sub in range(2):
                hl = hp * 2 + sub
                for c in range(8):
                    P.mm(pa[:, sub * 256:(sub + 1) * 256], w1[q].ap[:, c, hl * 128:(hl + 1) * 128], h2T.ap[:, c, fsl], c == 0, c == 7,
                         w1[q].b + [h2T.b[fb * 2], h2T.b[fb * 2 + 1]], [pb])
            r_ = rl[rl_c["i"] % 2]
            rl_c["i"] += 1
            P.act(r_.ap, pa, AF.Relu, [pb], r_.b)
            rv = r_.ap.rearrange("p (a b) -> p a b", a=2)
            P.tt("dve", f1.ap[:, hp * 2:(hp + 1) * 2, :], rv, rv, ALU.mult, r_.b, [f1.b[hp]])

    def ff2(fb, q):
        f1 = f1T[q % 2]
        for i in range(2):
            for half in range(2):
                acc, accb = ps[ACC[i * 2 + half]], psb[ACC[i * 2 + half]]
                for hc in range(8):
                    P.mm(acc, f1.ap[:, hc, i * 128:(i + 1) * 128], w2[q].ap[:, hc, half * 512:(half + 1) * 512],
                         q == 0 and hc == 0, q == 3 and hc == 7, w2[q].b + [f1.b[hc // 2]], [accb])

    def ffn_reload(fb):
        for i in range(2):
            tt = fb * 2 + i
            xs = st["xt"][tt % 3]
            P.dma("sp", xs.ap, x1s_d[tt * 128:(tt + 1) * 128, :], reads=[x1s_b[tt]], writes=xs.b, sem=f"xt{tt % 3}")

    def ffn_post(fb):
        for i in range(2):
            tt = fb * 2 + i
            tsl = slice(tt * 128, (tt + 1) * 128)
            xs = st["xt"][tt % 3]
            o_ = ot[tt % 2]
            post_norm_residual(ps[ACC[i * 2]], psb[ACC[i * 2]], ps[ACC[i * 2 + 1]], psb[ACC[i * 2 + 1]], gB, xs.ap, xs.b, o_)
            ev = P.dma("sp", out_d[tsl, :], o_.ap, reads=o_.b, writes=[], sem=f"out{tt % 2}")
            out_events.append(ev)

    rl_c = {"i": 0}
    steps = [(fb, q) for fb in range(8) for q in range(4)]
    ff1(*steps[0])
    for si, (fb, q) in enumerate(steps):
        if si + 1 < len(steps):
            ff1(*steps[si + 1])
        if q == 1:
            ffn_reload(fb)
        ff2(fb, q)
        if q == 3:
            ffn_post(fb)
    final = {}
    for k_, v_ in out_events + [ev for _, ev in dbg_list]:
        _merge(final, k_, v_)
    P.wait_all("sp", list(final.items()))
    P.emit(nc)
    return nc


_CACHE = {}


def _host_inputs(inputs):
    f32 = np.float32
    x = np.asarray(inputs["x"], f32)
    positions = np.asarray(inputs["positions"], np.int32)
    g = lambda k: np.ascontiguousarray(np.asarray(inputs[k], f32)[0])
    w_in = g("w_in")
    w_uq = g("w_uq")
    conv_w = g("conv_w")[:, 0, :]
    invf = (10000.0 ** (-np.arange(0, DR, 2, dtype=f32) / DR)).astype(f32)
    common = {}
    common["w_in"] = w_in
    common["wkr_sw"] = np.ascontiguousarray(np.concatenate([w_in[:, OFF_KR + 32:OFF_KR + 64], w_in[:, OFF_KR:OFF_KR + 32]], axis=1))
    common["w_uq"] = w_uq
    wq3 = w_uq.reshape(DQ, NH, 192)
    common["wuq_sw"] = np.ascontiguousarray(np.concatenate([wq3[:, :, 160:192], wq3[:, :, 128:160]], axis=2).reshape(DQ, NH * 64))
    common["w_uk"] = g("w_uk").reshape(DKV, NH * 128)
    common["w_uv"] = g("w_uv").reshape(DKV, NH * 128)
    common["w_o"] = g("w_o_attn")
    common["w_pw2"] = g("w_pw2")
    common["w_out"] = g("w_out")
    common["w_ff1"] = g("w_ff1")
    common["w_ff2"] = g("w_ff2")
    common["g_mix_pre"] = g("norm_mix_pre")[None, :]
    common["g_mix_post"] = g("norm_mix_post")[None, :]
    common["g_mlp_pre"] = g("norm_mlp_pre")[None, :]
    common["g_mlp_post"] = g("norm_mlp_post")[None, :]
    cst = np.zeros((128, 256), f32)
    cst[:, 0:128] = np.eye(128, dtype=f32)
    kk = np.arange(128)
    cst[:, 128:256] = (kk[:, None] <= kk[None, :]).astype(f32)
    common["cst"] = cst
    vecs = np.zeros((128, NV), f32)
    pm = lambda v: np.ascontiguousarray(np.asarray(v, f32).reshape(-1, 128).T)
    vecs[:, V_QN:V_QN + 3] = pm(g("q_norm"))
    vecs[:, V_KVN:V_KVN + 2] = pm(g("kv_norm"))
    vecs[:, V_CB:V_CB + 8] = pm(g("conv_b"))
    vecs[:, V_LNG:V_LNG + 8] = pm(g("conv_ln_g"))
    vecs[:, V_LNB:V_LNB + 8] = pm(g("conv_ln_b"))
    vecs[:, V_BPW:V_BPW + 8] = pm(g("b_pw2"))
    cwt = conv_w.T.reshape(8, 128, CW)
    vecs[:, V_CWT:V_CWT + 8 * CW] = np.transpose(cwt, (1, 0, 2)).reshape(128, 8 * CW)
    vecs[0:64, V_INVF] = np.concatenate([invf, invf])
    vecs[0:64, V_PHC] = np.pi / 2
    vecs[0:32, V_PHS] = np.pi
    vecs[32:64, V_PHS] = 0.0
    vecs[:, V_EPS] = EPS
    maps = []
    for core in range(8):
        b, r = core // 2, core % 2
        m = dict(common)
        m["x_own"] = np.ascontiguousarray(x[b, r * T:(r + 1) * T])
        m["x_pre"] = np.ascontiguousarray(x[b, 0:T])
        pp = np.concatenate([positions[b, 0:T], positions[b, r * T:(r + 1) * T]])[None, :]
        m["pos"] = np.ascontiguousarray(pp.astype(np.int32))
        v = vecs.copy()
        v[:, V_HALO] = float(r)
        v[:, V_KB:V_KB + 16] = 0.0 if r == 1 else -30000.0
        m["vecs"] = v
        maps.append(m)
    return maps


def kernel(**inputs):
    if "nc" not in _CACHE:
        dbg_list = []
        _CACHE["nc"] = build_program(dbg_list)
        _CACHE["dbg"] = [n for n, _ in dbg_list]
    nc = _CACHE["nc"]
    maps = _host_inputs(inputs)
    res = run_bass_kernel_spmd(nc, maps, core_ids=list(range(8)))
    out = np.empty((B, S, D), np.float32)
    for core in range(8):
        b, r = core // 2, core % 2
        out[b, r * T:(r + 1) * T] = res.results[core]["out"]
    if DEBUG:
        _CACHE["last"] = res.results
    return out
```

```python
import os
import numpy as np
import concourse.bass as bass
import concourse.mybir as mybir
from concourse.bass_utils import run_bass_kernel_spmd

F32 = mybir.dt.float32
BF16 = mybir.dt.bfloat16
I32 = mybir.dt.int32
AF = mybir.ActivationFunctionType
ALU = mybir.AluOpType

D = 1024
S = 4096
B = 4
T = 2048
NBLK = 4
DQ, DKV, DR = 384, 256, 64
NH = 8
DFF = 4096
EPS = 1e-6
CW = 31
DIN = 4800
OFF_Q, OFF_KV, OFF_KR, OFF_GLU, OFF_GATE = 0, 384, 640, 704, 2752
SCALE = 192.0 ** -0.5
MAGIC = 12582912.0
TWO_PI = 6.283185307179586
CW1 = 6.28125
CW2 = TWO_PI - CW1
PI_SAFE = 3.1415925

V_QN, V_KVN, V_CB, V_LNG, V_LNB, V_BPW, V_CWT = 0, 3, 5, 13, 21, 29, 37
V_INVF = 37 + 8 * CW
V_PHC, V_PHS, V_EPS, V_HALO, V_KB, V_ZERO = V_INVF + 1, V_INVF + 2, V_INVF + 3, V_INVF + 4, V_INVF + 5, V_INVF + 21
NV = V_ZERO + 1

DEBUG = bool(int(os.environ.get("MK_DEBUG", "0")))


class Buf:
    __slots__ = ("name", "w", "r")

    def __init__(self, name, ghost=None):
        self.name = name
        self.w = None
        self.r = dict(ghost) if ghost else {}


def _merge(d, key, val):
    if d.get(key, -1) < val:
        d[key] = val


class Prog:
    ENG = ["pe", "act", "dve", "pool", "sp"]

    def __init__(self):
        self.ops = {e: [] for e in self.ENG}
        self.waited = {e: {} for e in self.ENG}
        self.dma_cnt = {}

    def _add(self, eng, fn, reads, writes, dma_sem=None):
        idx = len(self.ops[eng])
        deps = []
        for b in reads:
            if b.w is not None:
                deps.append((b.w, "raw"))
        for b in writes:
            if b.w is not None:
                deps.append((b.w, "waw"))
            for k, v in b.r.items():
                deps.append(((k, v), "war"))
        waits = []
        for (key, val), kind in deps:
            if key == eng:
                if eng == "pe":
                    continue
            if self.waited[eng].get(key, -1) >= val:
                continue
            self.waited[eng][key] = val
            waits.append((key, val))
        if dma_sem is None:
            ev = (eng, idx)
        else:
            cnt = self.dma_cnt.get(dma_sem, 0) + 16
            self.dma_cnt[dma_sem] = cnt
            ev = ("dma:" + dma_sem, cnt)
        self.ops[eng].append(dict(fn=fn, waits=waits, dma=dma_sem, inc=False))
        for b in reads:
            _merge(b.r, ev[0], ev[1])
        for b in writes:
            b.w = ev
            b.r = {}
        return ev

    def op(self, eng, fn, reads=(), writes=()):
        return self._add(eng, fn, list(reads), list(writes))

    def dma(self, queue, out, in_, reads=(), writes=(), sem=None):
        assert sem is not None
        return self._add(queue, lambda e: e.dma_start(out=out, in_=in_), list(reads), list(writes), dma_sem=sem)

    def mm(self, out, lhsT, rhs, start, stop, reads, writes):
        return self._add("pe", lambda e: e.matmul(out, lhsT, rhs, start=start, stop=stop), list(reads), list(writes))

    def tr(self, out, in_, ident, reads, writes):
        return self._add("pe", lambda e: e.transpose(out, in_, ident), list(reads), list(writes))

    def act(self, out, in_, func, reads, writes, **kw):
        return self._add("act", lambda e: e.activation(out, in_, func, **kw), list(reads), list(writes))

    def tt(self, eng, out, in0, in1, op, reads, writes):
        return self._add(eng, lambda e: e.tensor_tensor(out, in0, in1, op), list(reads), list(writes))

    def ts(self, eng, out, in0, s1, s2, op0, op1, reads, writes):
        if op1 is None:
            return self._add(eng, lambda e: e.tensor_scalar(out, in0, s1, None, op0), list(reads), list(writes))
        return self._add(eng, lambda e: e.tensor_scalar(out, in0, s1, s2, op0, op1), list(reads), list(writes))

    def stt(self, out, in0, scalar, in1, op0, op1, reads, writes):
        return self._add("dve", lambda e: e.scalar_tensor_tensor(out, in0, scalar, in1, op0, op1), list(reads), list(writes))

    def recip(self, out, in_, reads, writes):
        return self._add("dve", lambda e: e.reciprocal(out, in_), list(reads), list(writes))

    def copy(self, eng, out, in_, reads, writes):
        if eng == "act":
            return self._add("act", lambda e: e.activation(out, in_, AF.Copy), list(reads), list(writes))
        return self._add(eng, lambda e: e.tensor_copy(out, in_), list(reads), list(writes))

    def wait_all(self, eng, events):
        waits = []
        for key, val in events:
            if self.waited[eng].get(key, -1) >= val:
                continue
            self.waited[eng][key] = val
            waits.append((key, val))
        self.ops[eng].append(dict(fn=None, waits=waits, dma=None, inc=False))

    def emit(self, nc):
        for e in self.ENG:
            for o in self.ops[e]:
                for key, val in o["waits"]:
                    if not key.startswith("dma:"):
                        self.ops[key][val]["inc"] = True
        cum = {}
        for e in self.ENG:
            c = 0
            arr = []
            for o in self.ops[e]:
                if o["inc"]:
                    c += 1
                arr.append(c)
            cum[e] = arr
        sems = {}
        import contextlib
        with contextlib.ExitStack() as st:
            for e in ["pe", "act", "dve", "pool"]:
                sems[e] = st.enter_context(nc.semaphore("s_" + e))
            for k in self.dma_cnt:
                sems["dma:" + k] = st.enter_context(nc.semaphore("d_" + k))
            block = st.enter_context(nc.Block())

            def run(ename):
                def f(eng):
                    for o in self.ops[ename]:
                        for key, val in o["waits"]:
                            v = val if key.startswith("dma:") else cum[key][val]
                            eng.wait_ge(sems[key], v)
                        if o["fn"] is None:
                            continue
                        ins = o["fn"](eng)
                        if o["dma"] is not None:
                            ins.then_inc(sems["dma:" + o["dma"]], 16)
                        elif o["inc"]:
                            ins.then_inc(sems[ename], 1)
                return f

            block.tensor(run("pe"))
            block.scalar(run("act"))
            block.vector(run("dve"))
            block.gpsimd(run("pool"))
            block.sync(run("sp"))


class SBAlloc:
    def __init__(self, big, nbytes):
        self.big = big
        self.free = [(0, nbytes)]
        self.ghosts = []
        self.live = {}

    def alloc(self, name, shape, dtype, top=False):
        dsz = 2 if dtype == BF16 else 4
        n = 1
        for s in shape[1:]:
            n *= s
        nbytes = (n * dsz + 63) // 64 * 64
        idxs = range(len(self.free) - 1, -1, -1) if top else range(len(self.free))
        for i in idxs:
            o, sz = self.free[i]
            if sz >= nbytes:
                if sz == nbytes:
                    off = o
                    self.free.pop(i)
                elif top:
                    off = o + sz - nbytes
                    self.free[i] = (o, sz - nbytes)
                else:
                    off = o
                    self.free[i] = (o + nbytes, sz - nbytes)
                break
        else:
            raise RuntimeError(f"SBUF OOM allocating {name} {nbytes}; free={self.free}")
        ghost = {}
        keep = []
        for (go, gs, gev) in self.ghosts:
            if go < off + nbytes and off < go + gs:
                for k, v in gev.items():
                    _merge(ghost, k, v)
            keep.append((go, gs, gev))
        self.ghosts = keep
        ap = self.big[:, off // 2: off // 2 + (n * dsz) // 2]
        if dtype != BF16:
            ap = ap.bitcast(dtype)
        if len(shape) == 3:
            ap = ap.rearrange("p (a b) -> p a b", a=shape[1])
        elif len(shape) == 4:
            ap = ap.rearrange("p (a b c) -> p a b c", a=shape[1], b=shape[2])
        if shape[0] < 128:
            ap = ap[0:shape[0]]
        self.live[name] = (off, nbytes)
        return ap, ghost

    def release(self, name, bufs):
        off, nbytes = self.live.pop(name)
        ev = {}
        for b in bufs:
            if b.w is not None:
                _merge(ev, b.w[0], b.w[1])
            for k, v in b.r.items():
                _merge(ev, k, v)
        self.ghosts.append((off, nbytes, ev))
        self.free.append((off, nbytes))
        self.free.sort()
        merged = []
        for o, s in self.free:
            if merged and merged[-1][0] + merged[-1][1] == o:
                merged[-1] = (merged[-1][0], merged[-1][1] + s)
            else:
                merged.append((o, s))
        self.free = merged


class Tn:
    def __init__(self, sb, name, shape, dtype, nbuf=1, top=True):
        self.sb = sb
        self.name = name
        self.ap, ghost = sb.alloc(name, shape, dtype, top=top)
        self.b = [Buf(f"{name}.{i}", ghost) for i in range(nbuf)]

    def free(self):
        self.sb.release(self.name, self.b)


def build_program(dbg_list):
    nc = bass.Bass("TRN2", target_bir_lowering=False)
    P = Prog()

    def din(name, shape, dt=F32):
        return nc.dram_tensor(name, list(shape), dt, kind="ExternalInput").ap()

    x_own = din("x_own", [T, D])
    x_pre = din("x_pre", [T, D])
    pos = din("pos", [1, S], I32)
    vecs_d = din("vecs", [128, NV])
    cst_d = din("cst", [128, 256])
    g_pre_d = din("g_mix_pre", [1, D])
    g_post_d = din("g_mix_post", [1, D])
    g_pre2_d = din("g_mlp_pre", [1, D])
    g_post2_d = din("g_mlp_post", [1, D])
    w_in_d = din("w_in", [D, DIN]).rearrange("(c p) n -> p c n", p=128)
    wkrsw_d = din("wkr_sw", [D, DR]).rearrange("(c p) n -> p c n", p=128)
    wuq_d = din("w_uq", [DQ, NH * 192]).rearrange("(c p) n -> p c n", p=128)
    wuqsw_d = din("wuq_sw", [DQ, NH * 64]).rearrange("(c p) n -> p c n", p=128)
    wuk_d = din("w_uk", [DKV, NH * 128]).rearrange("(c p) n -> p c n", p=128)
    wuv_d = din("w_uv", [DKV, NH * 128]).rearrange("(c p) n -> p c n", p=128)
    wo_d = din("w_o", [D, D]).rearrange("(c p) n -> p c n", p=128)
    wpw2_d = din("w_pw2", [D, D]).rearrange("(c p) n -> p c n", p=128)
    wout_d = din("w_out", [D, D]).rearrange("(c p) n -> p c n", p=128)
    wff1_d = din("w_ff1", [D, DFF]).rearrange("(c p) n -> p c n", p=128)
    wff2_d = din("w_ff2", [DFF, D]).rearrange("(c p) n -> p c n", p=128)
    out_d = nc.dram_tensor("out", [T, D], F32, kind="ExternalOutput").ap()
    x1s_d = nc.dram_tensor("x1s", [T, D], F32, kind="Internal").ap()
    x1s_b = [Buf(f"x1s.{i}") for i in range(16)]

    SB_BYTES = 212000
    big = nc.alloc_sbuf_tensor("big", [128, SB_BYTES // 2], BF16)[:, :]
    sb = SBAlloc(big, SB_BYTES)
    ps_t = [nc.alloc_psum_tensor(f"ps{i}", [128, 512], F32) for i in range(8)]
    ps = [t[:, :] for t in ps_t]
    psb = [Buf(f"ps{i}") for i in range(8)]
    rr = {"i": 0}

    def nps(pool=range(8)):
        pool = list(pool)
        k = rr.get(tuple(pool), 0)
        rr[tuple(pool)] = k + 1
        i = pool[k % len(pool)]
        return ps[i], psb[i]

    evc = {"i": 0}

    def evac_eng():
        evc["i"] += 1
        return "act" if evc["i"] % 2 else "dve"

    def copy_op(eng, out, in_, reads, writes):
        if eng == "act":
            P.op("act", lambda e: e.activation(out, in_, AF.Copy), reads, writes)
        else:
            P.op(eng, lambda e: e.tensor_copy(out, in_), reads, writes)

    dbg_cnt = {"i": 0}

    def dbg(name, tn_ap, bufs, shape, dt=F32):
        if not DEBUG:
            return
        d = nc.dram_tensor("dbg_" + name, list(shape), dt, kind="ExternalOutput").ap()
        dbg_cnt["i"] += 1
        ev = P.dma("sp", d, tn_ap, reads=bufs, writes=[], sem=f"dbg{dbg_cnt['i']}")
        dbg_list.append(("dbg_" + name, ev))

    vecs = Tn(sb, "vecs", [128, NV], F32, top=False)
    P.dma("sp", vecs.ap, vecs_d, writes=vecs.b, sem="vecs")
    cst = Tn(sb, "cst", [128, 256], BF16, top=False)
    P.dma("pool", cst.ap, cst_d, writes=cst.b, sem="cst")
    ident = cst.ap[:, 0:128]
    cmask = cst.ap[:, 128:256]
    ones = Tn(sb, "ones", [128, 128], BF16, top=False)
    P._add("pool", lambda e: e.memset(ones.ap, 1.0), [], ones.b)
    V = vecs.ap
    VB = vecs.b

    def vcol(c, rows=128):
        return V[0:rows, c:c + 1]

    ssc = Tn(sb, "ssc", [128, 8], F32, nbuf=8, top=False)
    gA = Tn(sb, "gA", [128, D], F32, top=False)
    tA = Tn(sb, "tA", [128, 512], F32, top=False)
    tB = Tn(sb, "tB", [128, 512], F32, top=False)
    sqb = [Tn(sb, f"sqb{i}", [128, 512], BF16, top=False) for i in range(3)]
    rstd = [Tn(sb, f"rstd{i}", [128, 512], F32, top=False) for i in range(2)]
    hT_halo = Tn(sb, "hT_halo", [128, 8, 32], BF16, top=False)
    hT_own = Tn(sb, "hT_own", [128, 8, T], BF16, nbuf=4, top=False)
    ropeC = Tn(sb, "ropeC", [64, T], F32, nbuf=4, top=False)
    ropeS = Tn(sb, "ropeS", [64, T], F32, nbuf=4, top=False)
    ckvT = Tn(sb, "ckvT", [128, 2, S], BF16, nbuf=8, top=False)
    kropeT = Tn(sb, "kropeT", [128, S], BF16, nbuf=8, top=False)
    P._add("pool", lambda e: e.memset(kropeT.ap[64:128, :], 0.0), [], kropeT.b)
    cqT = Tn(sb, "cqT", [128, 3, T], BF16, nbuf=4, top=False)
    hT_pre = Tn(sb, "hT_pre", [128, 8, T], BF16, nbuf=4)
    ropeCp = Tn(sb, "ropeCp", [64, T], F32, nbuf=4)
    ropeSp = Tn(sb, "ropeSp", [64, T], F32, nbuf=4)

    wkv = Tn(sb, "wkv", [128, 8, 320], BF16)
    P.dma("pool", wkv.ap, w_in_d[:, :, OFF_KV:OFF_KV + 320], writes=wkv.b, sem="wkv")
    wkrsw = Tn(sb, "wkrsw", [128, 8, 64], BF16)
    P.dma("pool", wkrsw.ap, wkrsw_d, writes=wkrsw.b, sem="wkrsw")
    wq = Tn(sb, "wq", [128, 8, DQ], BF16)
    P.dma("pool", wq.ap, w_in_d[:, :, OFF_Q:OFF_Q + DQ], writes=wq.b, sem="wq")
    P.dma("sp", gA.ap, g_pre_d[0:1, :].partition_broadcast(128), writes=gA.b, sem="gA")

    RC = 1024
    posb = Tn(sb, "posb", [64, RC], I32)
    posf = Tn(sb, "posf", [64, RC], F32)
    rtmp = Tn(sb, "rtmp", [64, RC], F32)
    rang = Tn(sb, "rang", [64, RC], F32)
    posi = Tn(sb, "posi", [64, RC], F32)
    pending_sin = {}

    def build_rope(ch):
        sl = slice(ch * RC, (ch + 1) * RC)
        lsl = slice((ch % 2) * RC, (ch % 2 + 1) * RC)
        P.dma("sp", posb.ap, pos[0:1, sl].partition_broadcast(64), writes=posb.b, sem="posb")
        P.copy("dve", posf.ap, posb.ap, posb.b, posf.b)
        tabs = (ropeCp, ropeSp) if ch < 2 else (ropeC, ropeS)
        sins = []
        for tab, phc in zip(tabs, (V_PHC, V_PHS)):
            tb = tab.b[(ch % 2) * 2:(ch % 2 + 1) * 2]
            dst = tab.ap[:, lsl]
            P.ts("dve", rang.ap, posf.ap, vcol(V_INVF, 64), vcol(phc, 64), ALU.mult, ALU.add, posf.b + VB, rang.b)
            P.ts("dve", rtmp.ap, rang.ap, 1.0 / TWO_PI, MAGIC, ALU.mult, ALU.add, rang.b, rtmp.b)
            P.ts("dve", rtmp.ap, rtmp.ap, -MAGIC, None, ALU.add, None, rtmp.b, rtmp.b)
            for cw_ in (CW1, CW2):
                P.stt(rang.ap, rtmp.ap, -cw_, rang.ap, ALU.mult, ALU.add, rtmp.b + rang.b, rang.b)
            P.ts("dve", dst, rang.ap, -PI_SAFE, PI_SAFE, ALU.max, ALU.min, rang.b, tb)
            sins.append((dst, tb))
        pending_sin[ch] = sins

    def flush_sin(ch):
        for dst, tb in pending_sin.pop(ch, []):
            P.act(dst, dst, AF.Sin, tb, tb)

    st = {}
    st["xt"] = [Tn(sb, f"xt{i}", [128, D], F32) for i in range(3)]
    st["xn"] = [Tn(sb, f"xn{i}", [128, D], BF16) for i in range(3)]
    st["junk"] = Tn(sb, "junk", [128, D], BF16)
    sc_i = {"i": 0}

    def scol():
        i = sc_i["i"] % 8
        sc_i["i"] += 1
        return ssc.ap[:, i:i + 1], ssc.b[i]

    def nt_front(src_ap, src_bufs, g_tn, k):
        junk = st["junk"]
        ss, ssb = scol()
        P.act(junk.ap, src_ap, AF.Square, src_bufs, junk.b + [ssb], accum_out=ss)
        P.act(ss, ss, AF.Ln, [ssb] + VB, [ssb], bias=vcol(V_EPS), scale=1.0 / D)
        P.act(ss, ss, AF.Exp, [ssb], [ssb], scale=-0.5)
        xnt = st["xn"][k % 3]
        P.stt(xnt.ap, src_ap, ss, g_tn.ap, ALU.mult, ALU.mult, list(src_bufs) + [ssb] + g_tn.b, xnt.b)

    def nt_back(dstT, dst_buf, tok0, k, pool=range(8)):
        xnt = st["xn"][k % 3]
        pa, pb = nps(pool)
        pav = pa.bitcast(BF16)
        for c in range(8):
            P.tr(pav[:, c * 128:(c + 1) * 128], xnt.ap[:, c * 128:(c + 1) * 128], ident, xnt.b + cst.b, [pb])
        P.copy(evac_eng(), dstT[:, :, tok0:tok0 + 128], pav.rearrange("p (c t) -> p c t", c=8), [pb], [dst_buf])

    def rstd_from_ss(bank, bankb, n, dst, dstb):
        P.act(dst, bank, AF.Ln, [bankb] + VB, [dstb], bias=vcol(V_EPS), scale=1.0 / n)
        P.act(dst, dst, AF.Exp, [dstb], [dstb], scale=-0.5)

    sq_i = {"i": 0}

    def next_sq():
        s_ = sqb[sq_i["i"] % 3]
        sq_i["i"] += 1
        return s_

    def lat_norm(banks, nchunk, gcol0, dstT, dst_buf, tsl, k):
        sqs = []
        for (ba, bb) in banks:
            s_ = next_sq()
            P.act(s_.ap, ba, AF.Square, [bb], s_.b)
            sqs.append(s_)
        sa, sbb = nps(MP)
        for j, s_ in enumerate(sqs):
            P.mm(sa, ones.ap, s_.ap, j == 0, j == nchunk - 1, s_.b + ones.b, [sbb])
        r = rstd[k % 2]
        rstd_from_ss(sa, sbb, nchunk * 128, r.ap, r.b[0])
        for j, (ba, bb) in enumerate(banks):
            P.stt(dstT[:, j, tsl], ba, vcol(gcol0 + j), r.ap, ALU.mult, ALU.mult, [bb] + r.b + VB, [dst_buf])

    def rope_apply(bank_r, bb_r, bank_s, bb_s, tC, tS, blk, dst, dst_buf):
        tsl_tab = slice(blk * 512, (blk + 1) * 512)
        P.tt("dve", tA.ap[0:64, :], bank_r[0:64, :], tC.ap[:, tsl_tab], ALU.mult, [bb_r, tC.b[blk]], tA.b)
        P.tt("dve", tB.ap[0:64, :], bank_s[0:64, :], tS.ap[:, tsl_tab], ALU.mult, [bb_s, tS.b[blk]], tB.b)
        P.tt("dve", dst, tA.ap[0:64, :], tB.ap[0:64, :], ALU.add, tA.b + tB.b, [dst_buf])

    tk = {"i": 0}
    MP = (0, 1, 2, 3, 4, 5)

    def tile_F(tk_):
        n, i = tk_ // 4, tk_ % 4
        own = n >= 4
        if tk_ % 8 == 0:
            build_rope(n // 2)
        src_d = x_own if own else x_pre
        t = (n % 4) * 4 + i
        xs = st["xt"][tk_ % 3]
        P.dma("sp", xs.ap, src_d[t * 128:(t + 1) * 128, :], writes=xs.b, sem=f"xt{tk_ % 3}")
        nt_front(xs.ap, xs.b, gA, tk_)

    def tile_B(tk_):
        n, i = tk_ // 4, tk_ % 4
        own = n >= 4
        hT = hT_own if own else hT_pre
        nb = n % 4
        t = nb * 4 + i
        nt_back(hT.ap, hT.b[nb], t * 128, tk_, pool=(6, 7))
        if tk_ == 15:
            P.copy("dve", hT_halo.ap, hT_pre.ap[:, :, T - 32:T], [hT_pre.b[3]], hT_halo.b)

    mstate = {}

    def M_kv_pe(n):
        own = n >= 4
        hT = hT_own if own else hT_pre
        nb = n % 4
        bsl = slice(nb * 512, (nb + 1) * 512)
        banks = []
        for j in range(2):
            pa, pb = nps(MP)
            for c in range(8):
                P.mm(pa, wkv.ap[:, c, j * 128:(j + 1) * 128], hT.ap[:, c, bsl], c == 0, c == 7, wkv.b + [hT.b[nb]], [pb])
            banks.append((pa, pb))
        pr, prb = nps(MP)
        for c in range(8):
            P.mm(pr[0:64, :], wkv.ap[:, c, 256:320], hT.ap[:, c, bsl], c == 0, c == 7, wkv.b + [hT.b[nb]], [prb])
        pq, pqb = nps(MP)
        for c in range(8):
            P.mm(pq[0:64, :], wkrsw.ap[:, c, :], hT.ap[:, c, bsl], c == 0, c == 7, wkrsw.b + [hT.b[nb]], [pqb])
        mstate[("kv", n)] = (banks, pr, prb, pq, pqb)

    def M_kv_post(n):
        own = n >= 4
        nb = n % 4
        asl = slice(n * 512, (n + 1) * 512)
        banks, pr, prb, pq, pqb = mstate.pop(("kv", n))
        flush_sin(n // 2)
        lat_norm(banks, 2, V_KVN, ckvT.ap, ckvT.b[n], asl, n)
        rope_apply(pr, prb, pq, pqb, ropeC if own else ropeCp, ropeS if own else ropeSp, nb, kropeT.ap[0:64, asl], kropeT.b[n])

    def M_q_pe(n):
        nb = n % 4
        bsl = slice(nb * 512, (nb + 1) * 512)
        banks = []
        for j in range(3):
            pa, pb = nps(MP)
            for c in range(8):
                P.mm(pa, wq.ap[:, c, j * 128:(j + 1) * 128], hT_own.ap[:, c, bsl], c == 0, c == 7, wq.b + [hT_own.b[nb]], [pb])
            banks.append((pa, pb))
        mstate[("q", n)] = banks

    def M_q_post(n):
        nb = n % 4
        bsl = slice(nb * 512, (nb + 1) * 512)
        lat_norm(mstate.pop(("q", n)), 3, V_QN, cqT.ap, cqT.b[nb], bsl, n + 1)

    sched = {}
    for n in range(8):
        base = 4 * n + 5
        sched.setdefault(base, []).append(lambda n=n: M_kv_pe(n))
        sched.setdefault(base + 1, []).append(lambda n=n: M_kv_post(n))
        if n >= 4:
            sched.setdefault(base + 1, []).append(lambda n=n: M_q_pe(n))
            sched.setdefault(base + 2, []).append(lambda n=n: M_q_post(n))
    tile_F(0)
    tile_F(1)
    for tk_ in range(32):
        if tk_ + 2 < 32:
            tile_F(tk_ + 2)
        tile_B(tk_)
        for f_ in sched.pop(tk_, []):
            f_()
    for k_ in sorted(sched):
        for f_ in sched[k_]:
            f_()
    dbg("hT_own", hT_own.ap, hT_own.b, [128, 8, T], BF16)
    dbg("ckvT", ckvT.ap, ckvT.b, [128, 2, S], BF16)
    dbg("kropeT", kropeT.ap[0:64, :], kropeT.b, [64, S], BF16)
    dbg("cqT", cqT.ap, cqT.b, [128, 3, T], BF16)
    hT_pre.free(); wkv.free(); wkrsw.free(); wq.free()
    for tn in st["xt"] + st["xn"] + [st["junk"], posb, posf, posi, rtmp, rang, ropeCp, ropeSp, gA] + sqb + rstd:
        tn.free()

    wuq = Tn(sb, "wuq", [128, 3, NH * 192], BF16)
    P.dma("pool", wuq.ap, wuq_d, writes=wuq.b, sem="wuq")
    wuqsw = Tn(sb, "wuqsw", [128, 3, NH * 64], BF16)
    P.dma("pool", wuqsw.ap, wuqsw_d, writes=wuqsw.b, sem="wuqsw")
    wuk = Tn(sb, "wuk", [128, 2, NH * 128], BF16)
    P.dma("pool", wuk.ap, wuk_d, writes=wuk.b, sem="wuk")
    wuv = Tn(sb, "wuv", [128, 2, NH * 128], BF16)
    P.dma("pool", wuv.ap, wuv_d, writes=wuv.b, sem="wuv")
    attnT = Tn(sb, "attnT", [128, NH, T], BF16, nbuf=NH * 4)
    KT2 = [Tn(sb, f"KT{i}", [128, S], BF16, nbuf=8) for i in range(2)]
    Vh2 = [Tn(sb, f"Vh{i}", [128, 32, 128], BF16, nbuf=8) for i in range(2)]
    QT2 = [Tn(sb, f"QT{i}", [128, T], BF16, nbuf=4) for i in range(2)]
    QrT2 = [Tn(sb, f"QrT{i}", [128, T], BF16, nbuf=4) for i in range(2)]
    for q_ in QrT2:
        P._add("pool", lambda e, q_=q_: e.memset(q_.ap[64:128, :], 0.0), [], q_.b)
    NPT = 6
    PT = [Tn(sb, f"PT{i}", [128, 512], BF16) for i in range(NPT)]
    rinv = Tn(sb, "rinv", [128, 512], F32)
    acc2 = [Tn(sb, f"acc{i}", [128, 512], F32) for i in range(2)]
    acch = Tn(sb, "acch", [128, 512], BF16)
    accl = Tn(sb, "accl", [128, 512], BF16)
    PREP = (6,)
    SBANK = (0, 1, 2)
    SUMB = (3, 7)

    def prep_groups(h, PREP=(6,)):
        KT, Vh, QT, QrT = KT2[h % 2], Vh2[h % 2], QT2[h % 2], QrT2[h % 2]
        gs = []

        def gK(n):
            pa, pb = nps(PREP)
            for j in range(2):
                P.mm(pa, wuk.ap[:, j, h * 128:(h + 1) * 128], ckvT.ap[:, j, n * 512:(n + 1) * 512], j == 0, j == 1, wuk.b + [ckvT.b[n]], [pb])
            P.copy("act", KT.ap[:, n * 512:(n + 1) * 512], pa, [pb], [KT.b[n]])

        def gV(g):
            pa, pb = nps(PREP)
            for i in range(4):
                kt = g * 4 + i
                for j in range(2):
                    P.mm(pa[:, i * 128:(i + 1) * 128], ckvT.ap[:, j, kt * 128:(kt + 1) * 128], wuv.ap[:, j, h * 128:(h + 1) * 128],
                         j == 0, j == 1, wuv.b + [ckvT.b[g]], [pb])
            P.copy("dve", Vh.ap[:, g * 4:(g + 1) * 4, :], pa.rearrange("p (a b) -> p a b", a=4), [pb], [Vh.b[g]])

        def gQ(b_):
            bsl = slice(b_ * 512, (b_ + 1) * 512)
            pa, pb = nps(PREP)
            for j in range(3):
                P.mm(pa, wuq.ap[:, j, h * 192:h * 192 + 128], cqT.ap[:, j, bsl], j == 0, j == 2, wuq.b + [cqT.b[b_]], [pb])
            P.copy("dve", QT.ap[:, bsl], pa, [pb], [QT.b[b_]])

        def gQr(b_, hb):
            hsl = slice(b_ * 512 + hb * 256, b_ * 512 + (hb + 1) * 256)
            pa, pb = nps(PREP)
            for j in range(3):
                P.mm(pa[0:64, 0:256], wuq.ap[:, j, h * 192 + 128:h * 192 + 192], cqT.ap[:, j, hsl], j == 0, j == 2, wuq.b + [cqT.b[b_]], [pb])
            for j in range(3):
                P.mm(pa[0:64, 256:512], wuqsw.ap[:, j, h * 64:(h + 1) * 64], cqT.ap[:, j, hsl], j == 0, j == 2, wuqsw.b + [cqT.b[b_]], [pb])
            P.tt("dve", tA.ap[0:64, 0:256], pa[0:64, 0:256], ropeC.ap[:, hsl], ALU.mult, [pb, ropeC.b[b_]], tA.b)
            P.tt("dve", tB.ap[0:64, 0:256], pa[0:64, 256:512], ropeS.ap[:, hsl], ALU.mult, [pb, ropeS.b[b_]], tB.b)
            P.tt("dve", QrT.ap[0:64, hsl], tA.ap[0:64, 0:256], tB.ap[0:64, 0:256], ALU.add, tA.b + tB.b, [QrT.b[b_]])

        for n in range(8):
            gs.append(lambda n=n: gK(n))
            gs.append(lambda n=n: gV(n))
        for b_ in range(4):
            gs.append(lambda b_=b_: gQ(b_))
            gs.append(lambda b_=b_: gQr(b_, 0))
            gs.append(lambda b_=b_: gQr(b_, 1))
        return gs

    its = []
    for h in range(NH):
        for qb in range(4):
            nfull = 16 + 4 * qb
            order = [0] + [nfull + j for j in range(4)] + list(range(1, nfull))
            for ii, kt in enumerate(order):
                its.append(dict(h=h, qb=qb, kt=kt, j=kt - nfull, ii=ii, last=(ii == len(order) - 1)))
    for k_, it in enumerate(its):
        it["pt"] = PT[k_ % NPT]
        it["s"] = SBANK[k_ % 3]

    def emit_S(it):
        h, qb, kt, j = it["h"], it["qb"], it["kt"], it["j"]
        KT, QT, QrT = KT2[h % 2], QT2[h % 2], QrT2[h % 2]
        q0 = j * 128 if j > 0 else 0
        nq = 512 - q0
        qsl = slice(qb * 512 + q0, qb * 512 + 512)
        sa, sbf = ps[it["s"]], psb[it["s"]]
        P.mm(sa[:, 0:nq], KT.ap[:, kt * 128:(kt + 1) * 128], QT.ap[:, qsl], True, False, [KT.b[kt // 4], QT.b[qb]], [sbf])
        P.mm(sa[:, 0:nq], kropeT.ap[:, kt * 128:(kt + 1) * 128], QrT.ap[:, qsl], False, True, [kropeT.b[kt // 4], QrT.b[qb]], [sbf])
        pt = it["pt"]
        bcol = vcol(V_KB + kt) if kt < 16 else vcol(V_ZERO)
        P.act(pt.ap[:, 0:nq], sa[:, 0:nq], AF.Exp, [sbf] + VB, pt.b, bias=bcol, scale=SCALE)
        if j >= 0:
            P.tt("dve", pt.ap[:, 0:128], pt.ap[:, 0:128], cmask, ALU.mult, pt.b + cst.b, pt.b)

    deferred = []

    def emit_SP(it):
        h, qb, kt, j, ii = it["h"], it["qb"], it["kt"], it["j"], it["ii"]
        Vh = Vh2[h % 2]
        q0 = j * 128 if j > 0 else 0
        nq = 512 - q0
        pt = it["pt"]
        psum_a, psum_b = ps[SUMB[qb % 2]], psb[SUMB[qb % 2]]
        po_a, po_b = ps[4 + qb % 2], psb[4 + qb % 2]
        acc = acc2[qb % 2]
        if ii % 4 != 1 or ii < 5:
            if ii == 0:
                P.copy("dve", acc.ap, pt.ap, pt.b, acc.b)
            else:
                P.tt("dve", acc.ap[:, q0:512], acc.ap[:, q0:512], pt.ap[:, 0:nq], ALU.add, acc.b + pt.b, acc.b)
        else:
            P.mm(psum_a[:, q0:512], ones.ap, pt.ap[:, 0:nq], ii == 5, False, pt.b + ones.b, [psum_b])
        P.mm(po_a[:, q0:512], Vh.ap[:, kt, :], pt.ap[:, 0:nq], ii == 0, it["last"], pt.b + [Vh.b[kt // 4]], [po_b])
        if it["last"]:
            def fin_dve(acc=acc):
                P.copy("dve", acch.ap, acc.ap, acc.b, acch.b)
                P.stt(accl.ap, acch.ap, -1.0, acc.ap, ALU.mult, ALU.add, acch.b + acc.b, accl.b)

            def fin(h=h, qb=qb, acc=acc, psum_a=psum_a, psum_b=psum_b, po_a=po_a, po_b=po_b):
                P.mm(psum_a, ones.ap, acch.ap, False, False, acch.b + ones.b, [psum_b])
                P.mm(psum_a, ones.ap, accl.ap, False, True, accl.b + ones.b, [psum_b])
                P.act(rinv.ap, psum_a, AF.Ln, [psum_b], rinv.b)
                P.act(rinv.ap, rinv.ap, AF.Exp, rinv.b, rinv.b, scale=-1.0)
                P.tt("dve", attnT.ap[:, h, qb * 512:(qb + 1) * 512], po_a, rinv.ap, ALU.mult, [po_b] + rinv.b, [attnT.b[h * 4 + qb]])
            deferred.append([3, fin_dve])
            deferred.append([9, fin])

    for g_ in prep_groups(0, PREP=(0, 1, 2, 3, 4, 5, 6, 7)):
        g_()
    if DEBUG:
        dbg("KT0", KT2[0].ap, KT2[0].b, [128, S], BF16)
        dbg("Vh0", Vh2[0].ap, Vh2[0].b, [128, 32, 128], BF16)
        dbg("QT0", QT2[0].ap, QT2[0].b, [128, T], BF16)
        dbg("QrT0", QrT2[0].ap[0:64, :], QrT2[0].b, [64, T], BF16)
    LOOK = 3
    pend = []
    cur_h = -1
    hcount = 0
    for k_ in range(min(LOOK, len(its))):
        emit_S(its[k_])
    for k_, it in enumerate(its):
        if it["h"] != cur_h:
            cur_h = it["h"]
            hcount = 0
            pend = prep_groups(cur_h + 1) if cur_h + 1 < NH else []
        emit_SP(it)
        if k_ + LOOK < len(its):
            emit_S(its[k_ + LOOK])
        for d_ in list(deferred):
            d_[0] -= 1
            if d_[0] <= 0:
                deferred.remove(d_)
                d_[1]()
        hcount += 1
        if pend and hcount >= 4 and hcount % 3 == 1:
            pend.pop(0)()
        if hcount == 100:
            while pend:
                pend.pop(0)()
    for d_ in deferred:
        d_[1]()
    dbg("attnT", attnT.ap, attnT.b, [128, NH, T], BF16)
    for tn in [wuq, wuqsw, wuk, wuv, rinv, ckvT, kropeT, cqT, ropeC, ropeS, acch, accl] + acc2 + KT2 + Vh2 + QT2 + QrT2 + PT:
        tn.free()

    wo = Tn(sb, "wo", [128, 8, D], BF16)
    P.dma("pool", wo.ap, wo_d, writes=wo.b, sem="wo")
    yaT = Tn(sb, "yaT", [128, 8, T], BF16, nbuf=32)
    for b_ in range(4):
        bsl = slice(b_ * 512, (b_ + 1) * 512)
        for dc in range(8):
            pa, pb = nps()
            for h in range(NH):
                P.mm(pa, wo.ap[:, h, dc * 128:(dc + 1) * 128], attnT.ap[:, h, bsl], h == 0, h == NH - 1, wo.b + [attnT.b[h * 4 + b_]], [pb])
            P.copy(evac_eng(), yaT.ap[:, dc, bsl], pa, [pb], [yaT.b[dc * 4 + b_]])
    attnT.free(); wo.free()
    dbg("yaT", yaT.ap, yaT.b, [128, 8, T], BF16)

    vT = Tn(sb, "vT", [128, 8, T], BF16, nbuf=32)
    uT = [Tn(sb, f"uT{i}", [128, 32 + T], BF16, nbuf=5) for i in range(2)]
    wglu = [Tn(sb, f"wglu{i}", [128, 8, 256], BF16) for i in range(2)]
    diag = [Tn(sb, f"diag{i}", [128, CW, 128], BF16) for i in range(2)]
    sg = [Tn(sb, f"sg{i}", [128, 512], F32) for i in range(2)]
    cacc = [Tn(sb, f"cacc{i}", [128, 512], F32) for i in range(2)]
    NDVE = 6
    for cc in range(8):
        k = cc % 2
        wg, ut, dg = wglu[k], uT[k], diag[k]
        P.dma("pool", wg.ap[:, :, 0:128], w_in_d[:, :, OFF_GLU + cc * 128:OFF_GLU + (cc + 1) * 128], writes=wg.b, sem=f"wglu{k}")
        P.dma("pool", wg.ap[:, :, 128:256], w_in_d[:, :, OFF_GLU + D + cc * 128:OFF_GLU + D + (cc + 1) * 128], writes=wg.b, sem=f"wglu{k}")
        c0 = V_CWT + cc * CW
        P.tt("dve", dg.ap, ident.unsqueeze(1).to_broadcast([128, CW, 128]), V[:, c0:c0 + CW].unsqueeze(2).to_broadcast([128, CW, 128]),
             ALU.mult, cst.b + VB, dg.b)
        pa, pb = nps()
        for half in range(2):
            for c in range(8):
                P.mm(pa[:, half * 32:(half + 1) * 32], wg.ap[:, c, half * 128:(half + 1) * 128], hT_halo.ap[:, c, :], c == 0, c == 7,
                     wg.b + hT_halo.b, [pb])
        s0 = sg[0]
        P.act(s0.ap[:, 0:32], pa[:, 32:64], AF.Sigmoid, [pb], s0.b)
        P.stt(ut.ap[:, 0:32], pa[:, 0:32], vcol(V_HALO), s0.ap[:, 0:32], ALU.mult, ALU.mult, [pb] + s0.b + VB, [ut.b[4]])
        for b_ in range(4):
            bsl = slice(b_ * 512, (b_ + 1) * 512)
            pa, pb = nps()
            pg, pgb = nps()
            for c in range(8):
                P.mm(pa, wg.ap[:, c, 0:128], hT_own.ap[:, c, bsl], c == 0, c == 7, wg.b + [hT_own.b[b_]], [pb])
            for c in range(8):
                P.mm(pg, wg.ap[:, c, 128:256], hT_own.ap[:, c, bsl], c == 0, c == 7, wg.b + [hT_own.b[b_]], [pgb])
            s_ = sg[b_ % 2]
            P.act(s_.ap, pg, AF.Sigmoid, [pgb], s_.b)
            P.tt("dve", ut.ap[:, 32 + b_ * 512:32 + (b_ + 1) * 512], pa, s_.ap, ALU.mult, [pb] + s_.b, [ut.b[b_]])
        for b_ in range(4):
            pa, pb = nps()
            rd = [ut.b[b_], ut.b[b_ - 1] if b_ > 0 else ut.b[4]]
            ca = cacc[b_ % 2]
            for j in range(NDVE):
                o = 32 + b_ * 512 - (CW - 1) + j
                wcol = vcol(V_CWT + cc * CW + j)
                if j == 0:
                    P.ts("dve", ca.ap, ut.ap[:, o:o + 512], wcol, None, ALU.mult, None, rd + VB, ca.b)
                else:
                    P.stt(ca.ap, ut.ap[:, o:o + 512], wcol, ca.ap, ALU.mult, ALU.add, rd + VB + ca.b, ca.b)
            for j in range(NDVE, CW):
                o = 32 + b_ * 512 - (CW - 1) + j
                P.mm(pa, dg.ap[:, j, :], ut.ap[:, o:o + 512], j == NDVE, j == CW - 1, dg.b + rd, [pb])
            P.stt(vT.ap[:, cc, b_ * 512:(b_ + 1) * 512], pa, vcol(V_CB + cc), ca.ap, ALU.add, ALU.add, [pb] + VB + ca.b, [vT.b[cc * 4 + b_]])
    dbg("yconv", vT.ap, vT.b, [128, 8, T], BF16)
    for tn in uT + wglu + diag + cacc:
        tn.free()
    wpw2 = Tn(sb, "wpw2", [128, 8, D], BF16)
    P.dma("pool", wpw2.ap, wpw2_d, writes=wpw2.b, sem="wpw2")
    wout = Tn(sb, "wout", [128, 8, D], BF16)
    meanf = [Tn(sb, f"meanf{i}", [128, 512], F32) for i in range(2)]
    sqb = [Tn(sb, f"sqb{i}", [128, 512], BF16) for i in range(3)]
    rstd = [Tn(sb, f"rstd{i}", [128, 512], F32) for i in range(2)]
    ln_r = {}

    def ln_stats(b_):
        bsl = slice(b_ * 512, (b_ + 1) * 512)
        p1, p1b = nps()
        p2, p2b = nps()
        for cc in range(8):
            s_ = next_sq()
            P.act(s_.ap, vT.ap[:, cc, bsl], AF.Square, [vT.b[cc * 4 + b_]], s_.b)
            P.mm(p1, ones.ap, vT.ap[:, cc, bsl], cc == 0, cc == 7, ones.b + [vT.b[cc * 4 + b_]], [p1b])
            P.mm(p2, ones.ap, s_.ap, cc == 0, cc == 7, ones.b + s_.b, [p2b])
        mf = meanf[b_ % 2]
        P.act(mf.ap, p1, AF.Identity, [p1b], mf.b, scale=1.0 / D)
        P.tt("dve", tA.ap, mf.ap, mf.ap, ALU.mult, mf.b, tA.b)
        r = rstd[b_ % 2]
        P.stt(r.ap, p2, 1.0 / D, tA.ap, ALU.mult, ALU.subtract, [p2b] + tA.b, r.b)
        P.act(r.ap, r.ap, AF.Ln, r.b + VB, r.b, bias=vcol(V_EPS))
        P.act(r.ap, r.ap, AF.Exp, r.b, r.b, scale=-0.5)

    def ln_apply(b_):
        bsl = slice(b_ * 512, (b_ + 1) * 512)
        mf, r = meanf[b_ % 2], rstd[b_ % 2]
        for cc in range(8):
            tt_ = sg[cc % 2]
            P.tt("dve", tt_.ap, vT.ap[:, cc, bsl], mf.ap, ALU.subtract, [vT.b[cc * 4 + b_]] + mf.b, tt_.b)
            P.tt("dve", tt_.ap, tt_.ap, r.ap, ALU.mult, tt_.b + r.b, tt_.b)
            P.act(vT.ap[:, cc, bsl], tt_.ap, AF.Silu, tt_.b + VB, [vT.b[cc * 4 + b_]], bias=vcol(V_LNB + cc), scale=vcol(V_LNG + cc))

    pf_state = {}
    def prefetch_wout():
        if pf_state:
            return
        pf_state["done"] = True
        P.dma("pool", wout.ap, wout_d, writes=wout.b, sem="wout")
        P.dma("sp", gB.ap, g_post_d[0:1, :].partition_broadcast(128), writes=gB.b, sem="gB")
        P.dma("sp", gA.ap, g_pre2_d[0:1, :].partition_broadcast(128), writes=gA.b, sem="gA")
    wga = Tn(sb, "wga", [128, 8, D], BF16)
    P.dma("pool", wga.ap, w_in_d[:, :, OFF_GATE:OFF_GATE + D], writes=wga.b, sem="wga")
    wgc = Tn(sb, "wgc", [128, 8, D], BF16)
    P.dma("pool", wgc.ap, w_in_d[:, :, OFF_GATE + D:OFF_GATE + 2 * D], writes=wgc.b, sem="wgc")
    sA = [Tn(sb, f"sA{i}", [128, 512], F32) for i in range(2)]
    sC = [Tn(sb, f"sC{i}", [128, 512], F32) for i in range(2)]
    gB = Tn(sb, "gB", [128, D], F32)
    gA = Tn(sb, "gA", [128, D], F32)
    prefetch_wout()

    def merge_block(b_):
        bsl = slice(b_ * 512, (b_ + 1) * 512)
        for dc in range(8):
            pga, pgab = nps()
            pgc, pgcb = nps()
            ppw, ppwb = nps()
            for c in range(8):
                P.mm(pga, wga.ap[:, c, dc * 128:(dc + 1) * 128], hT_own.ap[:, c, bsl], c == 0, c == 7, wga.b + [hT_own.b[b_]], [pgab])
            for c in range(8):
                P.mm(pgc, wgc.ap[:, c, dc * 128:(dc + 1) * 128], hT_own.ap[:, c, bsl], c == 0, c == 7, wgc.b + [hT_own.b[b_]], [pgcb])
            for c in range(8):
                P.mm(ppw, wpw2.ap[:, c, dc * 128:(dc + 1) * 128], vT.ap[:, c, bsl], c == 0, c == 7, wpw2.b + [vT.b[c * 4 + b_]], [ppwb])
            sa_, sc_ = sA[dc % 2], sC[dc % 2]
            P.act(sa_.ap, pga, AF.Sigmoid, [pgab], sa_.b)
            P.act(sc_.ap, pgc, AF.Sigmoid, [pgcb], sc_.b)
            P.stt(sc_.ap, ppw, vcol(V_BPW + dc), sc_.ap, ALU.add, ALU.mult, [ppwb] + sc_.b + VB, sc_.b)
            yb = yaT.b[dc * 4 + b_]
            P.tt("dve", sa_.ap, sa_.ap, yaT.ap[:, dc, bsl], ALU.mult, sa_.b + [yb], sa_.b)
            P.tt("dve", yaT.ap[:, dc, bsl], sa_.ap, sc_.ap, ALU.add, sa_.b + sc_.b, [yb])

    ln_stats(0)
    ln_stats(1)
    ln_apply(0)
    for b_ in range(4):
        if b_ + 2 < 4:
            ln_stats(b_ + 2)
        if b_ + 1 < 4:
            ln_apply(b_ + 1)
        merge_block(b_)
    dbg("vT", vT.ap, vT.b, [128, 8, T], BF16)
    dbg("mergedT", yaT.ap, yaT.b, [128, 8, T], BF16)
    for tn in [vT, wpw2, wga, wgc, hT_own, hT_halo] + meanf + sA + sC + sg + sqb + rstd:
        tn.free()

    h2T = Tn(sb, "h2T", [128, 8, T], BF16, nbuf=16)
    st["xt"] = [Tn(sb, f"xt{i}", [128, D], F32) for i in range(3)]
    st["xn"] = [Tn(sb, f"xn{i}", [128, D], BF16) for i in range(3)]
    st["junk"] = Tn(sb, "junk", [128, D], BF16)
    w1 = [Tn(sb, f"w1_{q}", [128, 8, 1024], BF16) for q in range(4)]
    for q in range(4):
        P.dma("pool", w1[q].ap, wff1_d[:, :, q * 1024:(q + 1) * 1024], writes=w1[q].b, sem=f"w1_{q}")
    x1t = [Tn(sb, f"x1t{i}", [128, D], F32) for i in range(4)]

    def post_norm_residual(pA, pAb, pB, pBb, g_tn, res_ap, res_bufs, dst):
        junk = st["junk"]
        s1, s1b = scol()
        s2, s2b = scol()
        P.act(junk.ap[:, 0:512], pA, AF.Square, [pAb], junk.b + [s1b], accum_out=s1)
        P.act(junk.ap[:, 512:1024], pB, AF.Square, [pBb], junk.b + [s2b], accum_out=s2)
        P.tt("dve", s1, s1, s2, ALU.add, [s1b, s2b], [s1b])
        P.act(s1, s1, AF.Ln, [s1b] + VB, [s1b], bias=vcol(V_EPS), scale=1.0 / D)
        P.act(s1, s1, AF.Exp, [s1b], [s1b], scale=-0.5)
        P.stt(dst.ap[:, 0:512], pA, s1, g_tn.ap[:, 0:512], ALU.mult, ALU.mult, [pAb, s1b] + g_tn.b, dst.b)
        P.stt(dst.ap[:, 512:1024], pB, s1, g_tn.ap[:, 512:1024], ALU.mult, ALU.mult, [pBb, s1b] + g_tn.b, dst.b)
        P.tt("pool", dst.ap, dst.ap, res_ap, ALU.add, dst.b + list(res_bufs), dst.b)

    def wo_A(tt):
        tsl = slice(tt * 128, (tt + 1) * 128)
        pA, pAb = nps((0, 1, 2, 3, 4, 5))
        pB, pBb = nps((0, 1, 2, 3, 4, 5))
        for half, (pp, ppb) in enumerate(((pA, pAb), (pB, pBb))):
            for c in range(8):
                P.mm(pp, yaT.ap[:, c, tsl], wout.ap[:, c, half * 512:(half + 1) * 512], c == 0, c == 7, wout.b + [yaT.b[c * 4 + tt // 4]], [ppb])
        xs = st["xt"][tt % 3]
        x1 = x1t[tt % 4]
        post_norm_residual(pA, pAb, pB, pBb, gB, xs.ap, xs.b, x1)
        P.dma("sp", x1s_d[tsl, :], x1.ap, reads=x1.b, writes=[x1s_b[tt]], sem=f"x1s{tt % 4}")

    def wo_F(tt):
        x1 = x1t[tt % 4]
        nt_front(x1.ap, x1.b, gA, tt)

    def wo_B(tt):
        nt_back(h2T.ap, h2T.b[tt], tt * 128, tt, pool=(6, 7))

    def wo_load(tt):
        xs = st["xt"][tt % 3]
        P.dma("sp", xs.ap, x_own[tt * 128:(tt + 1) * 128, :], writes=xs.b, sem=f"xt{tt % 3}")

    wo_load(0)
    wo_load(1)
    for s_ in range(16 + 3):
        if s_ + 2 < 16:
            wo_load(s_ + 2)
        if s_ < 16:
            wo_A(s_)
        if 0 <= s_ - 2 < 16:
            wo_F(s_ - 2)
        if 0 <= s_ - 3 < 16:
            wo_B(s_ - 3)
    dbg("h2T", h2T.ap, h2T.b, [128, 8, T], BF16)
    yaT.free(); wout.free()
    for tn in x1t + st["xn"] + [tA, tB, gA]:
        tn.free()

    w2 = [Tn(sb, f"w2_{q}", [128, 8, D], BF16) for q in range(4)]
    for q in range(4):
        P.dma("pool", w2[q].ap, wff2_d[:, q * 8:(q + 1) * 8, :], writes=w2[q].b, sem=f"w2_{q}")
    P.dma("sp", gB.ap, g_post2_d[0:1, :].partition_broadcast(128), writes=gB.b, sem="gB")
    f1T = [Tn(sb, f"f1T{i}", [128, 8, 256], BF16, nbuf=4) for i in range(2)]
    rl = [Tn(sb, f"rl{i}", [128, 512], F32) for i in range(2)]
    ot = [Tn(sb, f"ot{i}", [128, D], F32) for i in range(2)]
    ACC = (0, 1, 2, 3)
    FF1 = (4, 5, 6, 7)
    out_events = []
    rl_i = 0
    def ff1(fb, q):
        fsl = slice(fb * 256, (fb + 1) * 256)
        f1 = f1T[q % 2]
        for hp in range(4):
            pa, pb = nps(FF1)
            for sub in range(2):
                hl = hp * 2 + sub
                for c in range(8):
                    P.mm(pa[:, sub * 256:(sub + 1) * 256], w1[q].ap[:, c, hl * 128:(hl + 1) * 128], h2T.ap[:, c, fsl], c == 0, c == 7,
                         w1[q].b + [h2T.b[fb * 2], h2T.b[fb * 2 + 1]], [pb])
            r_ = rl[rl_c["i"] % 2]
            rl_c["i"] += 1
            P.act(r_.ap, pa, AF.Relu, [pb], r_.b)
            rv = r_.ap.rearrange("p (a b) -> p a b", a=2)
            P.tt("dve", f1.ap[:, hp * 2:(hp + 1) * 2, :], rv, rv, ALU.mult, r_.b, [f1.b[hp]])

    def ff2(fb, q):
        f1 = f1T[q % 2]
        for i in range(2):
            for half in range(2):
                acc, accb = ps[ACC[i * 2 + half]], psb[ACC[i * 2 + half]]
                for hc in range(8):
                    P.mm(acc, f1.ap[:, hc, i * 128:(i + 1) * 128], w2[q].ap[:, hc, half * 512:(half + 1) * 512],
                         q == 0 and hc == 0, q == 3 and hc == 7, w2[q].b + [f1.b[hc // 2]], [accb])

    def ffn_reload(fb):
        for i in range(2):
            tt = fb * 2 + i
            xs = st["xt"][tt % 3]
            P.dma("sp", xs.ap, x1s_d[tt * 128:(tt + 1) * 128, :], reads=[x1s_b[tt]], writes=xs.b, sem=f"xt{tt % 3}")

    def ffn_post(fb):
        for i in range(2):
            tt = fb * 2 + i
            tsl = slice(tt * 128, (tt + 1) * 128)
            xs = st["xt"][tt % 3]
            o_ = ot[tt % 2]
            post_norm_residual(ps[ACC[i * 2]], psb[ACC[i * 2]], ps[ACC[i * 2 + 1]], psb[ACC[i * 2 + 1]], gB, xs.ap, xs.b, o_)
            ev = P.dma("sp", out_d[tsl, :], o_.ap, reads=o_.b, writes=[], sem=f"out{tt % 2}")
            out_events.append(ev)

    rl_c = {"i": 0}
    steps = [(fb, q) for fb in range(8) for q in range(4)]
    ff1(*steps[0])
    for si, (fb, q) in enumerate(steps):
        if si + 1 < len(steps):
            ff1(*steps[si + 1])
        if q == 1:
            ffn_reload(fb)
        ff2(fb, q)
        if q == 3:
            ffn_post(fb)
    final = {}
    for k_, v_ in out_events + [ev for _, ev in dbg_list]:
        _merge(final, k_, v_)
    P.wait_all("sp", list(final.items()))
    P.emit(nc)
    return nc


_CACHE = {}


def _host_inputs(inputs):
    f32 = np.float32
    x = np.asarray(inputs["x"], f32)
    positions = np.asarray(inputs["positions"], np.int32)
    g = lambda k: np.ascontiguousarray(np.asarray(inputs[k], f32)[0])
    w_in = g("w_in")
    w_uq = g("w_uq")
    conv_w = g("conv_w")[:, 0, :]
    invf = (10000.0 ** (-np.arange(0, DR, 2, dtype=f32) / DR)).astype(f32)
    common = {}
    common["w_in"] = w_in
    common["wkr_sw"] = np.ascontiguousarray(np.concatenate([w_in[:, OFF_KR + 32:OFF_KR + 64], w_in[:, OFF_KR:OFF_KR + 32]], axis=1))
    common["w_uq"] = w_uq
    wq3 = w_uq.reshape(DQ, NH, 192)
    common["wuq_sw"] = np.ascontiguousarray(np.concatenate([wq3[:, :, 160:192], wq3[:, :, 128:160]], axis=2).reshape(DQ, NH * 64))
    common["w_uk"] = g("w_uk").reshape(DKV, NH * 128)
    common["w_uv"] = g("w_uv").reshape(DKV, NH * 128)
    common["w_o"] = g("w_o_attn")
    common["w_pw2"] = g("w_pw2")
    common["w_out"] = g("w_out")
    common["w_ff1"] = g("w_ff1")
    common["w_ff2"] = g("w_ff2")
    common["g_mix_pre"] = g("norm_mix_pre")[None, :]
    common["g_mix_post"] = g("norm_mix_post")[None, :]
    common["g_mlp_pre"] = g("norm_mlp_pre")[None, :]
    common["g_mlp_post"] = g("norm_mlp_post")[None, :]
    cst = np.zeros((128, 256), f32)
    cst[:, 0:128] = np.eye(128, dtype=f32)
    kk = np.arange(128)
    cst[:, 128:256] = (kk[:, None] <= kk[None, :]).astype(f32)
    common["cst"] = cst
    vecs = np.zeros((128, NV), f32)
    pm = lambda v: np.ascontiguousarray(np.asarray(v, f32).reshape(-1, 128).T)
    vecs[:, V_QN:V_QN + 3] = pm(g("q_norm"))
    vecs[:, V_KVN:V_KVN + 2] = pm(g("kv_norm"))
    vecs[:, V_CB:V_CB + 8] = pm(g("conv_b"))
    vecs[:, V_LNG:V_LNG + 8] = pm(g("conv_ln_g"))
    vecs[:, V_LNB:V_LNB + 8] = pm(g("conv_ln_b"))
    vecs[:, V_BPW:V_BPW + 8] = pm(g("b_pw2"))
    cwt = conv_w.T.reshape(8, 128, CW)
    vecs[:, V_CWT:V_CWT + 8 * CW] = np.transpose(cwt, (1, 0, 2)).reshape(128, 8 * CW)
    vecs[0:64, V_INVF] = np.concatenate([invf, invf])
    vecs[0:64, V_PHC] = np.pi / 2
    vecs[0:32, V_PHS] = np.pi
    vecs[32:64, V_PHS] = 0.0
    vecs[:, V_EPS] = EPS
    maps = []
    for core in range(8):
        b, r = core // 2, core % 2
        m = dict(common)
        m["x_own"] = np.ascontiguousarray(x[b, r * T:(r + 1) * T])
        m["x_pre"] = np.ascontiguousarray(x[b, 0:T])
        pp = np.concatenate([positions[b, 0:T], positions[b, r * T:(r + 1) * T]])[None, :]
        m["pos"] = np.ascontiguousarray(pp.astype(np.int32))
        v = vecs.copy()
        v[:, V_HALO] = float(r)
        v[:, V_KB:V_KB + 16] = 0.0 if r == 1 else -30000.0
        m["vecs"] = v
        maps.append(m)
    return maps


def kernel(**inputs):
    if "nc" not in _CACHE:
        dbg_list = []
        _CACHE["nc"] = build_program(dbg_list)
        _CACHE["dbg"] = [n for n, _ in dbg_list]
    nc = _CACHE["nc"]
    maps = _host_inputs(inputs)
    res = run_bass_kernel_spmd(nc, maps, core_ids=list(range(8)))
    out = np.empty((B, S, D), np.float32)
    for core in range(8):
        b, r = core // 2, core % 2
        out[b, r * T:(r + 1) * T] = res.results[core]["out"]
    if DEBUG:
        _CACHE["last"] = res.results
    return out
```
